# Optimizing a Trainium2 kernel written in Bass

```python
import jax, jax.numpy as jnp
from jax import lax
import numpy as np

D_MODEL = 1024
BATCH = 32
SEQ = 2048
DEPTH = 1
DEC_BATCH = 2
DEC_SEQ = 8192
PAST_LEN = 128

N_FOURIER_GROUPS = 4
FOURIER_GROUP_DIM = 128
FOURIER_WIDTH = N_FOURIER_GROUPS * FOURIER_GROUP_DIM
N_SGU_GROUPS = 4
SGU_GROUP_DIM = 128
SGU_WIDTH = N_SGU_GROUPS * SGU_GROUP_DIM
CHUNK = 128
D_FF = 4 * D_MODEL
CONV_WIDTH = 3
IN_PROJ_WIDTH = FOURIER_WIDTH + 2 * SGU_WIDTH + 2 * D_MODEL
EPS = 1e-6

kernel_name = "fnet_gmlp_gated_hybrid_encoder"


def rmsnorm(x, g):
    xf = x.astype(jnp.float32)
    r = lax.rsqrt(jnp.mean(xf * xf, axis=-1, keepdims=True) + EPS)
    return (xf * r * g.astype(jnp.float32)).astype(x.dtype)


def layernorm(x, g, b):
    xf = x.astype(jnp.float32)
    mu = jnp.mean(xf, axis=-1, keepdims=True)
    var = jnp.mean(jnp.square(xf - mu), axis=-1, keepdims=True)
    y = (xf - mu) * lax.rsqrt(var + EPS) * g.astype(jnp.float32) + b.astype(jnp.float32)
    return y.astype(x.dtype)


def fourier_mix(f):
    b, s, _ = f.shape
    z = f.reshape(b, s, N_FOURIER_GROUPS, FOURIER_GROUP_DIM).astype(jnp.float32)
    z = jnp.fft.fft2(z, axes=(1, 3), norm="ortho").real
    return z.reshape(b, s, FOURIER_WIDTH).astype(f.dtype)


def spatial_gating(u, v, ln_g, ln_b, w_s, b_s):
    b, s, _ = v.shape
    v = layernorm(v, ln_g, ln_b)
    vc = v.reshape(b, s // CHUNK, CHUNK, N_SGU_GROUPS, SGU_GROUP_DIM)
    mixed = jnp.einsum('hpq,bnqhc->bnphc', w_s, vc) + b_s.T[:, :, None]
    return u * mixed.reshape(b, s, SGU_WIDTH)


def centred_depthwise_conv(x, w, bias):
    xp = jnp.pad(x, ((0, 0), (1, 1), (0, 0)))
    return xp[:, :-2] * w[0] + xp[:, 1:-1] * w[1] + xp[:, 2:] * w[2] + bias


def layer(x, norm_pre_mix, w_in, sgu_ln_g, sgu_ln_b, sgu_w_s, sgu_b_s,
          w_fourier_out, w_sgu_out, w_o, norm_post_mix, norm_pre_ffn,
          w_up, conv_w, conv_b, w_down, norm_post_ffn):
    h = rmsnorm(x, norm_pre_mix)
    proj = jnp.einsum('bsd,de->bse', h, w_in)
    o1 = FOURIER_WIDTH
    o2 = o1 + SGU_WIDTH
    o3 = o2 + SGU_WIDTH
    o4 = o3 + D_MODEL
    f = proj[..., :o1]
    u = jax.nn.gelu(proj[..., o1:o2])
    v = jax.nn.gelu(proj[..., o2:o3])
    gate_a = jax.nn.sigmoid(proj[..., o3:o4])
    gate_b = jax.nn.sigmoid(proj[..., o4:])
    y_a = jnp.einsum('bsc,cd->bsd', fourier_mix(f), w_fourier_out)
    y_b = jnp.einsum('bsc,cd->bsd', spatial_gating(u, v, sgu_ln_g, sgu_ln_b, sgu_w_s, sgu_b_s), w_sgu_out)
    merged = gate_a * y_a + gate_b * y_b
    x = x + rmsnorm(jnp.einsum('bsd,de->bse', merged, w_o), norm_post_mix)

    h2 = rmsnorm(x, norm_pre_ffn)
    up = jnp.einsum('bsd,df->bsf', h2, w_up)
    up = centred_depthwise_conv(up, conv_w, conv_b)
    act = jax.nn.gelu(up[..., :D_FF]) * up[..., D_FF:]
    ff = jnp.einsum('bsf,fd->bsd', act, w_down)
    return x + rmsnorm(ff, norm_post_ffn)


def setup_inputs(seed: int = 0) -> dict:
    key = jax.random.key(seed)
    ks = jax.random.split(key, 20)
    f32 = jnp.float32

    def nrm(k, shape, scale):
        return jax.random.normal(k, shape, f32) * scale

    def gain(k, n):
        return jnp.ones((DEPTH, n), f32) + nrm(k, (DEPTH, n), 0.02)

    return {
        "x_prompt": nrm(ks[0], (BATCH, SEQ, D_MODEL), 1.0),
        "x_sample": nrm(ks[1], (DEC_BATCH, DEC_SEQ, D_MODEL), 1.0),
        "norm_pre_mix": gain(ks[2], D_MODEL),
        "w_in": nrm(ks[3], (DEPTH, D_MODEL, IN_PROJ_WIDTH), D_MODEL ** -0.5),
        "sgu_ln_g": gain(ks[4], SGU_WIDTH),
        "sgu_ln_b": nrm(ks[5], (DEPTH, SGU_WIDTH), 0.02),
        "sgu_w_s": nrm(ks[6], (DEPTH, N_SGU_GROUPS, CHUNK, CHUNK), CHUNK ** -0.5),
        "sgu_b_s": nrm(ks[7], (DEPTH, N_SGU_GROUPS, CHUNK), 0.02),
        "w_fourier_out": nrm(ks[8], (DEPTH, FOURIER_WIDTH, D_MODEL), FOURIER_WIDTH ** -0.5),
        "w_sgu_out": nrm(ks[9], (DEPTH, SGU_WIDTH, D_MODEL), SGU_WIDTH ** -0.5),
        "w_o": nrm(ks[10], (DEPTH, D_MODEL, D_MODEL), D_MODEL ** -0.5),
        "norm_post_mix": gain(ks[11], D_MODEL),
        "norm_pre_ffn": gain(ks[12], D_MODEL),
        "w_up": nrm(ks[13], (DEPTH, D_MODEL, 2 * D_FF), D_MODEL ** -0.5),
        "conv_w": nrm(ks[14], (DEPTH, CONV_WIDTH, 2 * D_FF), CONV_WIDTH ** -0.5),
        "conv_b": nrm(ks[15], (DEPTH, 2 * D_FF), 0.02),
        "w_down": nrm(ks[16], (DEPTH, D_FF, D_MODEL), D_FF ** -0.5),
        "norm_post_ffn": gain(ks[17], D_MODEL),
    }


def reference(x_prompt, x_sample, norm_pre_mix, w_in, sgu_ln_g, sgu_ln_b, sgu_w_s, sgu_b_s,
              w_fourier_out, w_sgu_out, w_o, norm_post_mix, norm_pre_ffn,
              w_up, conv_w, conv_b, w_down, norm_post_ffn):
    y_prompt = x_prompt
    y_sample = x_sample
    for l in range(DEPTH):
        p = (norm_pre_mix[l], w_in[l], sgu_ln_g[l], sgu_ln_b[l], sgu_w_s[l], sgu_b_s[l],
             w_fourier_out[l], w_sgu_out[l], w_o[l], norm_post_mix[l], norm_pre_ffn[l],
             w_up[l], conv_w[l], conv_b[l], w_down[l], norm_post_ffn[l])
        y_prompt = layer(y_prompt, *p)
        y_sample = layer(y_sample, *p)
    return (y_prompt, y_sample)
```

```python
import numpy as np
import concourse.bass as bass
import concourse.mybir as mybir
from concourse.bass_utils import run_bass_kernel_spmd

_BF16NP = mybir.dt.np(mybir.dt.bfloat16)

F32 = mybir.dt.float32
BF16 = mybir.dt.bfloat16
AF = mybir.ActivationFunctionType
ALU = mybir.AluOpType

D = 1024
NTOK = 10240
S8 = 8192
S2 = 2048
BLK = 512
NBLK = NTOK // BLK
DFF = 4096
EPS = 1e-6
NCORES = 8

ENGS = ("pe", "act", "dve", "pool", "sp")
NDMA_SEMS = 44
NDMA_HW = 32


class Op:
    __slots__ = ("eng", "fn", "deps", "signal", "count", "is_dma", "ndma", "idx", "eidx", "sem", "semval")


class Prog:
    def __init__(self, nc):
        self.nc = nc
        self.ops = []
        self.eng_ops = {e: [] for e in ENGS}
        self.last_writer = {}
        self.readers = {}
        self.seen = {e: {} for e in ENGS}
        self.seen_dma = {e: set() for e in ENGS}
        self.dma_last = [None] * NDMA_SEMS
        self.dma_use = [0] * NDMA_SEMS
        self.dma_rr = 0
        self.dma_rr_sw = 0

    def op(self, eng, fn, reads=(), writes=(), ndma=0):
        o = Op()
        o.eng = eng
        o.fn = fn
        o.signal = False
        o.count = 0
        o.is_dma = ndma > 0
        o.ndma = ndma
        o.idx = len(self.ops)
        o.eidx = len(self.eng_ops[eng])
        o.sem = None
        o.semval = 0
        deps = {}
        for r in reads:
            w = self.last_writer.get(r)
            if w is not None and not (w.eng == "pe" and eng == "pe" and not w.is_dma):
                deps[w.idx] = w
            self.readers.setdefault(r, []).append(o)
        for r in writes:
            w = self.last_writer.get(r)
            if w is not None and w is not o and not (w.eng == "pe" and eng == "pe" and not w.is_dma):
                deps[w.idx] = w
            for rd in self.readers.get(r, ()):
                if rd is not o:
                    deps[rd.idx] = rd
            self.last_writer[r] = o
            self.readers[r] = []
        if o.is_dma:
            if eng == "pool":
                s = NDMA_HW + self.dma_rr_sw
                self.dma_rr_sw = (self.dma_rr_sw + 1) % (NDMA_SEMS - NDMA_HW)
            else:
                s = self.dma_rr
                self.dma_rr = (self.dma_rr + 1) % NDMA_HW
            prev = self.dma_last[s]
            if prev is not None:
                deps[prev.idx] = prev
            self.dma_use[s] += 16 * ndma
            o.sem = s
            o.semval = self.dma_use[s]
            self.dma_last[s] = o
        keep = []
        best = {}
        for d in deps.values():
            if d.is_dma:
                if d.idx in self.seen_dma[eng]:
                    continue
                self.seen_dma[eng].add(d.idx)
                keep.append(d)
            else:
                if self.seen[eng].get(d.eng, -1) >= d.eidx:
                    continue
                if d.eng not in best or best[d.eng].eidx < d.eidx:
                    best[d.eng] = d
        for e, d in best.items():
            self.seen[eng][e] = d.eidx
            keep.append(d)
        for d in keep:
            d.signal = True
        o.deps = keep
        self.ops.append(o)
        self.eng_ops[eng].append(o)
        return o

    def barrier(self):
        dmas = [d for d in self.dma_last if d is not None]
        for e in ENGS:
            o = Op()
            o.eng = e
            o.fn = None
            o.signal = False
            o.count = 0
            o.is_dma = False
            o.ndma = 0
            o.idx = len(self.ops)
            o.eidx = len(self.eng_ops[e])
            o.sem = None
            o.semval = 0
            keep = []
            for e2 in ENGS:
                lst = self.eng_ops[e2]
                k = len(lst) - 1
                while k >= 0 and (lst[k].is_dma or lst[k].fn is None):
                    k -= 1
                if k >= 0 and e2 != e:
                    d = lst[k]
                    if self.seen[e].get(e2, -1) < d.eidx:
                        self.seen[e][e2] = d.eidx
                        keep.append(d)
            for d in dmas:
                if d.idx not in self.seen_dma[e]:
                    self.seen_dma[e].add(d.idx)
                    keep.append(d)
            for d in keep:
                d.signal = True
            o.deps = keep
            self.ops.append(o)
            self.eng_ops[e].append(o)
        self.last_writer = {}
        self.readers = {}

    def emit(self, final_wait_engine="sp"):
        nc = self.nc
        tail = [d for d in self.dma_last if d is not None]
        import contextlib
        with contextlib.ExitStack() as es:
            eng_sem = {e: es.enter_context(nc.semaphore("sem_" + e)) for e in ENGS}
            dma_sems = [es.enter_context(nc.semaphore("dsem%d" % i)) for i in range(NDMA_SEMS)]
            for e in ENGS:
                c = 0
                for o in self.eng_ops[e]:
                    if o.signal and not o.is_dma:
                        c += 1
                        o.count = c
            block = es.enter_context(nc.Block())

            def run(ename, eng):
                for o in self.eng_ops[ename]:
                    for d in o.deps:
                        if d.is_dma:
                            eng.wait_ge(dma_sems[d.sem], d.semval)
                        else:
                            eng.wait_ge(eng_sem[d.eng], d.count)
                    if o.fn is None:
                        assert not o.signal
                        continue
                    if o.is_dma:
                        o.fn(eng, dma_sems[o.sem])
                    else:
                        ins = o.fn(eng)
                        if o.signal:
                            ins.then_inc(eng_sem[ename], 1)
                if ename == final_wait_engine:
                    for d in tail:
                        eng.wait_ge(dma_sems[d.sem], d.semval)

            block.tensor(lambda eng: run("pe", eng))
            block.scalar(lambda eng: run("act", eng))
            block.vector(lambda eng: run("dve", eng))
            block.gpsimd(lambda eng: run("pool", eng))
            block.sync(lambda eng: run("sp", eng))


class Arena:
    def __init__(self, nc, nbytes):
        self.t = nc.alloc_sbuf_tensor("arena", [128, nbytes // 4], F32)
        self.nbytes = nbytes
        self.off = 0

    def mark(self):
        return self.off

    def release(self, m):
        self.off = m

    def alloc(self, n, dtype):
        nb = n * (2 if dtype == BF16 else 4)
        nb = (nb + 31) // 32 * 32
        assert self.off + nb <= self.nbytes, ("arena overflow", self.off, nb, self.nbytes)
        ap = self.t[:, self.off // 4:(self.off + nb) // 4]
        self.off += nb
        if dtype == BF16:
            ap = ap.bitcast(BF16)
        return ap[:, 0:n]


C_GPM, C_GPF = 0, 8
C_GPOM = 16
C_GPOF = C_GPOM + 1024
C_CW = C_GPOF + 1024
C_CB = C_CW + 192
C_LNG = C_CB + 64
C_LNB = C_LNG + 4
C_BS = C_LNB + 4
C_WST = C_BS + 512
C_HM = C_WST + 512
NCST = C_HM + 1
NCST = (NCST + 7) // 8 * 8
B_ID = 0
B_CS = 128
B_M8 = B_CS + 512
B_M2 = B_M8 + 256
B_WST = B_M2 + 256
NCSTB = B_WST + 512


def _dft_tables(kind):
    rho = np.arange(128)[:, None, None].astype(np.float64)
    kap = np.arange(128)[None, None, :].astype(np.float64)
    tau = np.arange(64)[None, :, None].astype(np.float64)
    if kind == "sample":
        psi = 2 * np.pi * (rho * kap / 128.0 + tau * kap / 8192.0)
    else:
        psi = 2 * np.pi * (rho * kap / 128.0 + (tau % 16) * kap / 2048.0)
    sc = 1.0 / np.sqrt(128.0)
    t8 = np.concatenate([np.cos(psi) * sc, -np.sin(psi) * sc], axis=2)
    t = np.arange(64)
    if kind == "sample":
        th = 2 * np.pi * np.outer(t, t) / 64.0
        mc = np.cos(th) / 8.0
        ms = np.sin(th) / 8.0
    else:
        same = (t[:, None] // 16) == (t[None, :] // 16)
        th = 2 * np.pi * np.outer(t % 16, t % 16) / 16.0
        mc = np.where(same, np.cos(th), 0.0) / 4.0
        ms = np.where(same, np.sin(th), 0.0) / 4.0
    eye2 = np.eye(2)
    m8c = np.kron(mc, eye2)
    m8s = np.kron(ms, eye2)
    tau2 = np.arange(16)[None, :, None].astype(np.float64)
    psi2 = 2 * np.pi * (rho * kap / 128.0 + tau2 * kap / 2048.0)
    t2 = np.concatenate([np.cos(psi2) * sc, -np.sin(psi2) * sc], axis=2)
    t16 = np.arange(16)
    th2 = 2 * np.pi * np.outer(t16, t16) / 16.0
    eye8 = np.eye(8)
    m2c = np.kron(np.cos(th2) / 4.0, eye8)
    m2s = np.kron(np.sin(th2) / 4.0, eye8)
    c = np.arange(128)
    ph = 2 * np.pi * np.outer(c, c) / 128.0
    C = np.cos(ph) * sc
    S = np.sin(ph) * sc
    cs = np.concatenate([C, -S, S, C], axis=1)
    return (t8.reshape(128, 64 * 256).astype(_BF16NP), t2.reshape(128, 16 * 256).astype(_BF16NP),
            cs.astype(np.float32), np.concatenate([m8c, m8s], axis=1).astype(np.float32),
            np.concatenate([m2c, m2s], axis=1).astype(np.float32))


def _fourier_row_order(kind):
    idx = np.empty(NTOK, dtype=np.int64)
    tau = np.arange(64)[:, None]
    rho = np.arange(128)[None, :]
    if kind == "sample":
        tok = tau + 64 * rho
    else:
        tok = 2048 * (tau // 16) + (tau % 16) + 16 * rho
    idx[:S8] = tok.reshape(-1)
    tau2 = np.arange(16)[:, None]
    idx[S8:] = (S8 + tau2 + 16 * rho).reshape(-1)
    return idx


def _halo_mask(kind):
    hm = np.zeros((128, 1), dtype=np.float32)
    if kind == "sample":
        bounds = [0, S8, NTOK]
    else:
        bounds = list(range(0, NTOK + 1, 2048))
    for i in range(NBLK):
        t0 = i * BLK
        hm[i, 0] = 0.0 if t0 in bounds else 1.0
        hm[NBLK + i, 0] = 0.0 if (t0 + BLK) in bounds else 1.0
    return hm


def build_program(cfg=None):
    cfg = cfg or {}
    MUL_ENG = cfg.get("mul_eng", "pool")
    ADD_ENG = cfg.get("add_eng", "pool")
    phases = cfg.get("phases", "ABC")
    dbg = cfg.get("debug", False)
    nc = bass.Bass("TRN2", target_bir_lowering=False)

    def din(name, shape):
        return nc.dram_tensor(name, shape, F32, kind="ExternalInput").ap()

    xn = din("xn", [NTOK, D])
    xf = din("xf", [NTOK, D])
    w_in = din("w_in", [D, 3584])
    w_fo = din("w_fo", [512, D])
    w_so = din("w_so", [512, D])
    w_o = din("w_o", [D, D])
    w_up = din("w_up", [D, 2 * DFF])
    w_down = din("w_down", [DFF, D])
    cst_d = din("cst", [128, NCST])
    cstb_d = din("cstb", [128, NCSTB])
    t8_d = nc.dram_tensor("t8", [128, 64 * 256], BF16, kind="ExternalInput").ap()
    t2_d = nc.dram_tensor("t2", [128, 16 * 256], BF16, kind="ExternalInput").ap()
    y = nc.dram_tensor("y", [NTOK, D], F32, kind="ExternalOutput").ap()
    kscr = "ExternalOutput" if dbg else "Internal"
    yscr = nc.dram_tensor("yscr", [4, 128, NTOK], BF16, kind=kscr).ap()
    x1s = nc.dram_tensor("x1s", [NTOK, D], F32, kind=kscr).ap()
    wups = nc.dram_tensor("wups", [32, 128, 2048], BF16, kind="Internal").ap()

    P = Prog(nc)
    arena = Arena(nc, 212000)
    psum = nc.alloc_psum_tensor("psum", [128, 4096], F32)

    def bank(b, n=1):
        return psum[:, 512 * b:512 * (b + n)]

    def bankb(b):
        return psum[:, 512 * b:512 * (b + 1)].bitcast(BF16)

    cst = arena.alloc(NCST, F32)
    cstb = arena.alloc(NCSTB, BF16)
    neghalf = arena.alloc(8, F32)
    ssq = [arena.alloc(8, F32) for _ in range(4)]
    rsd = [arena.alloc(8, F32) for _ in range(4)]
    ssq2 = [arena.alloc(8, F32) for _ in range(4)]
    rsd2 = [arena.alloc(8, F32) for _ in range(4)]
    ident = cstb[:, B_ID:B_ID + 128]
    gpm = cst[:, C_GPM:C_GPM + 8]
    gpf = cst[:, C_GPF:C_GPF + 8]

    def dma1(out, in_, **kw):
        def f(eng, sem):
            eng.dma_start(out=out, in_=in_, **kw).then_inc(sem, 16)
        return f

    P.op("sp", dma1(cst, cst_d), writes=["cst"], ndma=1)
    P.op("pool", dma1(cstb, cstb_d), writes=["cstb"], ndma=1)
    P.op("pool", lambda e: e.memset(neghalf, -0.5), writes=["neghalf"])
    for i in range(4):
        P.op("pool", lambda e, i=i: e.memset(ssq[i], 1.0), writes=[("ssq", i, t) for t in range(8)])
    sqjunks = [arena.alloc(D, BF16) for _ in range(2)]

    def cast_wup():
        wv = w_up.rearrange("(kc p) (h j f) -> j p kc h f", p=128, h=2, f=128)
        for j in range(32):
            ov = wups[j].rearrange("p (kc h f) -> p kc h f", kc=8, h=2)

            def f(eng, sem, j=j, ov=ov):
                for h in range(2):
                    eng.dma_start(out=ov[:, :, h, :], in_=wv[j][:, :, h, :]).then_inc(sem, 16)
            P.op("pool", f, writes=[("wups", j)], ndma=2)

    ring_ctr = {}

    def ring(name, n):
        c = ring_ctr.get(name, 0)
        ring_ctr[name] = c + 1
        return c % n

    def pre_a(tag, tiles, xt_bufs, eps, hm_col=None):
        assert len(xt_bufs) >= len(tiles)
        s = ring(tag + "ss", 4)
        ss, rs = ssq[s], rsd[s]
        st = []
        for ti, (Pn, src, dst, dres) in enumerate(tiles):
            k = ring(tag + "xt", len(xt_bufs))
            xt = xt_bufs[k]
            st.append((Pn, src, dst, dres, k))
            pieces = src if isinstance(src, list) else [(0, Pn, src)]

            def ld(eng, sem, xt=xt, pieces=pieces):
                for (p0, p1, ap) in pieces:
                    eng.dma_start(out=xt[p0:p1, :], in_=ap).then_inc(sem, 16)
            P.op("sp", ld, writes=[(tag + "xt", k)], ndma=len(pieces))
        for ti, (Pn, src, dst, dres, k) in enumerate(st):
            xt = xt_bufs[k]
            jk = ring("sqjunk", 2)
            P.op("act", lambda e, xt=xt, Pn=Pn, ti=ti, ss=ss, jk=jk: e.activation(
                out=sqjunks[jk][:Pn, :], in_=xt[:Pn, :], func=AF.Square, accum_out=ss[:Pn, ti:ti + 1]),
                reads=[(tag + "xt", k)], writes=[("sqjunk", jk), ("ssq", s, ti)])
        nt = len(tiles)
        P.op("pool", lambda e: e.tensor_scalar(out=rs[:, 0:nt], in0=ss[:, 0:nt], scalar1=1.0 / D, scalar2=eps,
                                               op0=ALU.mult, op1=ALU.add),
             reads=[("ssq", s, ti) for ti in range(nt)], writes=[("rsd", s)])
        P.op("pool", lambda e: e.tensor_tensor(out=rs[:, 0:nt], in0=rs[:, 0:nt], in1=neghalf[:, 0:nt], op=ALU.pow),
             reads=[("rsd", s), "neghalf"], writes=[("rsd", s)])
        if hm_col is not None:
            P.op("pool", lambda e: e.tensor_tensor(out=rs[:, nt - 1:nt], in0=rs[:, nt - 1:nt], in1=hm_col,
                                                   op=ALU.mult),
                 reads=[("rsd", s), "cst"], writes=[("rsd", s)])
        return (tag, st, s, xt_bufs)

    def pre_b(state, xs_bufs, tbank, gvec):
        tag, st, s, xt_bufs = state
        rs = rsd[s]
        for ti, (Pn, src, dst, dres, k) in enumerate(st):
            xt = xt_bufs[k]
            k2 = ring(tag + "xs", len(xs_bufs))
            xs = xs_bufs[k2]
            P.op("act", lambda e, xt=xt, xs=xs, Pn=Pn, ti=ti: e.activation(
                out=xs[:Pn, :], in_=xt[:Pn, :], func=AF.Copy, scale=rs[:Pn, ti:ti + 1]),
                reads=[(tag + "xt", k), ("rsd", s)], writes=[(tag + "xs", k2)])
            tbk = tbank[ti % len(tbank)] if isinstance(tbank, tuple) else tbank
            tb = bankb(tbk)

            def tr(e, xs=xs, Pn=Pn, tb=tb):
                ins = None
                for kc in range(8):
                    ins = e.transpose(out=tb[:, kc * 128:kc * 128 + Pn], in_=xs[:Pn, kc * 128:(kc + 1) * 128],
                                      identity=ident[:Pn, :Pn])
                return ins
            P.op("pe", tr, reads=[(tag + "xs", k2), "cstb"], writes=[("ps", tbk)])
            tb3 = tb.rearrange("p (k t) -> p k t", t=128)
            P.op("dve", lambda e, dst=dst, tb3=tb3, Pn=Pn: e.tensor_tensor(
                out=dst, in0=tb3[:, :, 0:Pn], in1=gvec.unsqueeze(2).broadcast_to([128, 8, Pn]), op=ALU.mult),
                reads=[("ps", tbk), "cst"], writes=[dres])

    def mm_group(out, pairs, reads, writes):
        def f(e):
            ins = None
            n = len(pairs)
            for i, (l, r) in enumerate(pairs):
                ins = e.matmul(out, lhsT=l, rhs=r, start=(i == 0), stop=(i == n - 1))
            return ins
        return P.op("pe", f, reads=reads, writes=writes)

    def evac(eng, out, in_, reads, writes):
        if eng == "act":
            return P.op("act", lambda e: e.activation(out=out, in_=in_, func=AF.Copy), reads=reads, writes=writes)
        return P.op("dve", lambda e: e.tensor_copy(out=out, in_=in_), reads=reads, writes=writes)

    base_mark = arena.mark()

    def both(name):
        return [(name, "act"), (name, "dve")]

    alt = [0]

    def alt_eng():
        alt[0] ^= 1
        return "act" if alt[0] else "dve"

    def phase_a():
        arena.release(base_mark)
        tcs = arena.alloc(64 * 256, BF16)
        Fb = arena.alloc(64 * 512, BF16).rearrange("p (t c) -> p t c", c=512)
        Gb0 = arena.alloc(2 * 64 * 128, BF16)
        YT = arena.alloc(S8, BF16)
        Bb = [arena.alloc(256, BF16) for _ in range(4)]
        ovl_mark = arena.mark()
        wf = arena.alloc(8 * 512, BF16).rearrange("p (k c) -> p k c", c=512)
        xt = [arena.alloc(D, F32) for _ in range(4)]
        xs = [arena.alloc(D, BF16) for _ in range(2)]
        hfT = [arena.alloc(8 * 256, BF16).rearrange("p (k t) -> p k t", t=256) for _ in range(2)]
        end_mark = arena.mark()
        arena.release(ovl_mark)
        Gb1 = arena.alloc(2 * 64 * 128, BF16)
        assert arena.mark() <= end_mark
        arena.release(end_mark)
        ovl_res = ["wf"] + [("axt", k) for k in range(4)] + [("axs", k) for k in range(2)] + \
                  [("hfT", k) for k in range(2)]
        cs1 = cstb[:, B_CS:B_CS + 256]
        cs2 = cstb[:, B_CS + 256:B_CS + 512]

        P.op("pool", dma1(wf, w_in[:, 0:512].rearrange("(k p) c -> p k c", p=128)), writes=["wf"], ndma=1)

        for (tok0, NT1, t_d, mcol, Gbufs) in ((S8, 16, t2_d, B_M2, [Gb0]), (0, 64, t8_d, B_M8, [Gb0, Gb1])):
            E = 128 // NT1
            NB = NT1
            ncol = NT1 * 256
            P.op("sp", dma1(tcs[:, 0:ncol], t_d), writes=["tcs"], ndma=1)
            tc3 = tcs[:, 0:ncol].rearrange("p (t k) -> p t k", k=256)
            mc = cstb[:, mcol:mcol + 128]
            ms = cstb[:, mcol + 128:mcol + 256]
            G5s = [G[:, 0:2 * NB * 128].rearrange("p (a b t e) -> p a b t e", a=2, b=NB, t=NT1, e=E) for G in Gbufs]
            G3s = [G[:, 0:2 * NB * 128].rearrange("p (a b m) -> p a b m", a=2, b=NB) for G in Gbufs]
            YT4 = YT[:, 0:NT1 * 128].rearrange("p (k b e) -> p k b e", b=NB, e=E)
            ng = len(Gbufs)

            def tiles_for(pi):
                hb = hfT[pi % 2]
                res = ("hfT", pi % 2)
                return [(128, xf[tok0 + (2 * pi + u) * 128:tok0 + (2 * pi + u + 1) * 128, :],
                         hb[:, :, u * 128:(u + 1) * 128], res) for u in range(2)]
            npair = NT1 // 2
            sts = {0: pre_a("a", tiles_for(0), xt, EPS)}
            pre_b(sts[0], xs, (0, 7), gpm)
            if npair > 1:
                sts[1] = pre_a("a", tiles_for(1), xt, EPS)
            for pi in range(npair):
                if pi + 1 < npair:
                    pre_b(sts[pi + 1], xs, (0, 7), gpm)
                if pi + 2 < npair:
                    sts[pi + 2] = pre_a("a", tiles_for(pi + 2), xt, EPS)
                hb = hfT[pi % 2]
                for u in range(2):
                    tau = 2 * pi + u
                    fb = (1, 2)[ring("fbank", 2)]
                    mm_group(bank(fb), [(hb[:, kc, u * 128:(u + 1) * 128], wf[:, kc, :]) for kc in range(8)],
                             reads=[("hfT", pi % 2), "wf"], writes=[("ps", fb)])
                    evac("dve", Fb[:, tau, :], bank(fb), reads=[("ps", fb)], writes=[("F", "dve")])

            if tok0 == 0 and "C" in phases:
                cast_wup()

            def stage1(g, tau):
                gi = g % ng
                sb = (3, 4, 2)[ring("s1bank", 3)]
                mm_group(bank(sb)[:, 0:256], [(Fb[:, tau, g * 128:(g + 1) * 128], tc3[:, tau, :])],
                         reads=[("F", "dve"), "tcs"], writes=[("ps", sb)])
                srcv = bank(sb)[:, 0:256].rearrange("p (a b e) -> p a b e", a=2, b=NB, e=E)
                extra = ovl_res if (gi == 1) else []
                evac("dve", G5s[gi][:, :, :, tau, :], srcv, reads=[("ps", sb)], writes=[("G", gi)] + extra)

            state = {"pend": None, "s3cur": None}

            def stage23(g, b):
                gi = g % ng
                if b < NB:
                    s2b = (5, 6)[ring("s2bank", 2)]
                    mm_group(bank(s2b)[:, 0:256], [(G3s[gi][:, 0, b, :], cs1), (G3s[gi][:, 1, b, :], cs2)],
                             reads=[("G", gi), "cstb"], writes=[("ps", s2b)])
                    k = ring("Bb", 4)
                    evac("act", Bb[k], bank(s2b)[:, 0:256], reads=[("ps", s2b)], writes=[("Bb", k)])
                if state["pend"] is not None:
                    pb, pk = state["pend"]
                    q4 = pb % 4
                    if q4 == 0:
                        state["s3cur"] = (7, 1)[ring("s3bank", 2)]
                    s3cur = state["s3cur"]
                    mm_group(bank(s3cur)[:, q4 * 128:(q4 + 1) * 128],
                             [(Bb[pk][:, 0:128], mc), (Bb[pk][:, 128:256], ms)],
                             reads=[("Bb", pk), "cstb"], writes=[("ps", s3cur)])
                    if q4 == 3:
                        b4 = pb // 4
                        srcv = bank(s3cur).rearrange("p (b k e) -> p b k e", b=4, k=NT1, e=E)
                        dstv = YT4[:, :, 4 * b4:4 * b4 + 4, :].rearrange("p k b e -> p b k e")
                        evac("dve", dstv, srcv, reads=[("ps", s3cur)], writes=[("YT", "dve")])
                state["pend"] = (b, k) if b < NB else None

            if ng == 1:
                for g in range(4):
                    for tau in range(NT1):
                        stage1(g, tau)
                    for b in range(NB + 1):
                        stage23(g, b)
                    P.op("sp", dma1(yscr[g][:, tok0:tok0 + NT1 * 128], YT[:, 0:NT1 * 128]),
                         reads=[("YT", "dve")], writes=[("yscr", g, tok0)], ndma=1)
            else:
                for tau in range(NT1):
                    stage1(0, tau)
                for g in range(4):
                    for b in range(NB + 1):
                        stage23(g, b)
                        if g + 1 < 4 and b < NT1:
                            stage1(g + 1, b)
                    P.op("sp", dma1(yscr[g][:, tok0:tok0 + NT1 * 128], YT[:, 0:NT1 * 128]),
                         reads=[("YT", "dve")], writes=[("yscr", g, tok0)], ndma=1)

    def phase_b():
        arena.release(base_mark)
        nblk = cfg.get("nblk_b", NBLK)
        win = arena.alloc(8 * 3072, BF16).rearrange("p (k c) -> p k c", c=3072)
        wfo = arena.alloc(4 * 1024, BF16).rearrange("p (g d) -> p g d", d=1024)
        wso = arena.alloc(4 * 1024, BF16).rearrange("p (g d) -> p g d", d=1024)
        wo = arena.alloc(8 * 1024, BF16).rearrange("p (k d) -> p k d", d=1024)
        bmat = arena.alloc(512, F32)
        ones = arena.alloc(128, BF16)
        xt = [arena.alloc(D, F32) for _ in range(4)]
        xs = [arena.alloc(D, BF16) for _ in range(2)]
        hT = arena.alloc(8 * 512, BF16).rearrange("p (k t) -> p k t", t=512)
        uT = arena.alloc(4 * 512, BF16).rearrange("p (c t) -> p c t", t=512)
        taT = arena.alloc(8 * 512, BF16).rearrange("p (c t) -> p c t", t=512)
        tbT = arena.alloc(8 * 512, BF16).rearrange("p (c t) -> p c t", t=512)
        vg = [arena.alloc(512, F32) for _ in range(2)]
        vhat = arena.alloc(4 * 512, BF16).rearrange("p (t c) -> p t c", c=512)
        st6 = [arena.alloc(8, F32) for _ in range(2)]
        mv = [arena.alloc(8, F32) for _ in range(2)]
        Yb = [arena.alloc(4 * 512, BF16).rearrange("p (g t) -> p g t", t=512) for _ in range(2)]
        sT = arena.alloc(4 * 512, BF16).rearrange("p (h t) -> p h t", t=512)
        mT = arena.alloc(8 * 512, BF16).rearrange("p (k t) -> p k t", t=512)
        tmp = [arena.alloc(512, F32) for _ in range(4)]
        tt = [arena.alloc(D, F32) for _ in range(2)]
        xr = [arena.alloc(D, F32) for _ in range(2)]
        gpom = cst[:, C_GPOM:C_GPOM + 1024]
        wstb = cstb[:, B_WST:B_WST + 512].rearrange("p (h q) -> p h q", q=128)

        def ld_win(eng, sem):
            for q in range(3):
                eng.dma_start(out=win[:, :, q * 1024:(q + 1) * 1024],
                              in_=w_in[:, 512 + q * 1024:512 + (q + 1) * 1024].rearrange("(k p) c -> p k c", p=128)
                              ).then_inc(sem, 16)
        P.op("pool", ld_win, writes=["win"], ndma=3)
        P.op("pool", dma1(wfo, w_fo.rearrange("(g p) d -> p g d", p=128)), writes=["wfo"], ndma=1)
        P.op("pool", dma1(wso, w_so.rearrange("(g p) d -> p g d", p=128)), writes=["wso"], ndma=1)
        P.op("pool", dma1(wo, w_o.rearrange("(k p) d -> p k d", p=128)), writes=["wo"], ndma=1)
        P.op("pool", lambda e: e.memset(ones, 1.0), writes=["ones"])
        mm_group(bank(1), [(ones, cstb[:, B_WST:B_WST + 512])], reads=["ones", "cstb"], writes=[("ps", 1)])
        for h in range(4):
            P.op("dve", lambda e, h=h: e.scalar_tensor_tensor(
                out=bmat[:, h * 128:(h + 1) * 128], in0=bank(1)[:, h * 128:(h + 1) * 128],
                scalar=cst[:, C_LNB + h:C_LNB + h + 1], in1=cst[:, C_BS + h * 128:C_BS + (h + 1) * 128],
                op0=ALU.mult, op1=ALU.add), reads=[("ps", 1), "cst"], writes=["bmat"])

        def tiles_for(i):
            t0 = i * BLK
            return [(128, xn[t0 + t * 128:t0 + (t + 1) * 128, :], hT[:, :, t * 128:(t + 1) * 128], "hT")
                    for t in range(4)]

        wring = (1, 2)
        ypairs = ((4, 5), (6, 7))
        U0, V0, GA0, GB0 = 0, 512, 1024, 2048
        st = pre_a("b", tiles_for(0), xt, EPS)
        pre_b(st, xs, (0, 3), gpm)
        def b5(i):
            t0 = i * BLK
            for t in range(4):
                yp = ypairs[ring("ypair", 2)]
                k2 = ring("xr", 2)
                rows = slice(t0 + t * 128, t0 + (t + 1) * 128)
                P.op("sp", dma1(xr[k2], xn[rows, :]), writes=[("xr", k2)], ndma=1)
                for dh in range(2):
                    mm_group(bank(yp[dh]), [(mT[:, kc, t * 128:(t + 1) * 128], wo[:, kc, dh * 512:(dh + 1) * 512])
                                            for kc in range(8)],
                             reads=[("mT", kc) for kc in range(8)] + ["wo"], writes=[("ps", yp[dh])])
                s2 = ring("ss2", 4)
                po = bank(yp[0], 2)
                jk = ring("sqjunk", 2)
                P.op("act", lambda e, po=po, s2=s2, jk=jk: e.activation(out=sqjunks[jk], in_=po, func=AF.Square,
                                                                       accum_out=ssq2[s2][:, 0:1]),
                     reads=[("ps", yp[0]), ("ps", yp[1])], writes=[("sqjunk", jk), ("ssq2", s2)])
                P.op("pool", lambda e, s2=s2: e.tensor_scalar(out=rsd2[s2][:, 0:1], in0=ssq2[s2][:, 0:1],
                                                             scalar1=1.0 / D, scalar2=4.0 * EPS, op0=ALU.mult,
                                                             op1=ALU.add),
                     reads=[("ssq2", s2)], writes=[("rsd2", s2)])
                P.op("pool", lambda e, s2=s2: e.tensor_tensor(out=rsd2[s2][:, 0:1], in0=rsd2[s2][:, 0:1],
                                                             in1=neghalf[:, 0:1], op=ALU.pow),
                     reads=[("rsd2", s2), "neghalf"], writes=[("rsd2", s2)])
                k = ring("tt", 2)
                P.op("dve", lambda e, po=po, s2=s2, k=k: e.scalar_tensor_tensor(
                    out=tt[k], in0=po, scalar=rsd2[s2][:, 0:1], in1=gpom, op0=ALU.mult, op1=ALU.mult),
                    reads=[("ps", yp[0]), ("ps", yp[1]), ("rsd2", s2), "cst"], writes=[("tt", k)])
                P.op(ADD_ENG, lambda e, k=k, k2=k2: e.tensor_tensor(out=tt[k], in0=tt[k], in1=xr[k2], op=ALU.add),
                     reads=[("tt", k), ("xr", k2)], writes=[("tt", k)])
                P.op("sp", dma1(x1s[rows, :], tt[k]), reads=[("tt", k)], writes=[("x1s", i, t)], ndma=1)


        for i in range(nblk):
            t0 = i * BLK
            yb = Yb[i % 2]
            P.op("sp", dma1(yb, yscr[:, :, t0:t0 + BLK].rearrange("g p t -> p g t")), reads=[("yscr",)],
                 writes=[("Yb", i % 2)], ndma=1)
            for t in range(4):
                wb = wring[ring("wring", 2)]
                mm_group(bank(wb), [(hT[:, kc, t * 128:(t + 1) * 128], win[:, kc, V0:V0 + 512]) for kc in range(8)],
                         reads=["hT", "win"], writes=[("ps", wb)])
                k = ring("vg", 2)
                P.op("act", lambda e, k=k, wb=wb: e.activation(out=vg[k], in_=bank(wb), func=AF.Gelu_apprx_tanh),
                     reads=[("ps", wb)], writes=[("vg", k)])
                P.op("dve", lambda e, k=k: e.bn_stats(out=st6[k][:, 0:6], in_=vg[k]), reads=[("vg", k)],
                     writes=[("st6", k)])
                P.op("dve", lambda e, k=k: e.bn_aggr(out=mv[k][:, 0:2], in_=st6[k][:, 0:6]), reads=[("st6", k)],
                     writes=[("mv", k)])
                P.op("pool", lambda e, k=k: e.tensor_scalar(out=mv[k][:, 2:3], in0=mv[k][:, 1:2], scalar1=1.0,
                                                           scalar2=EPS, op0=ALU.mult, op1=ALU.add),
                     reads=[("mv", k)], writes=[("mvr", k)])
                P.op("pool", lambda e, k=k: e.tensor_tensor(out=mv[k][:, 2:3], in0=mv[k][:, 2:3],
                                                           in1=neghalf[:, 0:1], op=ALU.pow),
                     reads=[("mvr", k), "neghalf"], writes=[("mvr", k)])
                P.op("dve", lambda e, k=k, t=t: e.tensor_scalar(out=vhat[:, t, :], in0=vg[k], scalar1=mv[k][:, 0:1],
                                                               scalar2=mv[k][:, 2:3], op0=ALU.subtract,
                                                               op1=ALU.mult),
                     reads=[("vg", k), ("mv", k), ("mvr", k)], writes=[("vhat", t)])
            if i > 0:
                b5(i - 1)
            def proj_chunk(off, c, dst, dname, func, scale):
                wb = wring[ring("wring", 2)]
                mm_group(bank(wb), [(win[:, kc, off + c * 128:off + (c + 1) * 128], hT[:, kc, :])
                                    for kc in range(8)],
                         reads=["hT", "win"], writes=[("ps", wb)])
                P.op("act", lambda e: e.activation(out=dst[:, c, :], in_=bank(wb), func=func, scale=scale),
                     reads=[("ps", wb)], writes=[(dname, c)])

            def sgu_group(h):
                wb = wring[ring("wring", 2)]
                for t in range(4):
                    mm_group(bank(wb)[:, t * 128:(t + 1) * 128],
                             [(vhat[:, t, h * 128:(h + 1) * 128], wstb[:, h, :])],
                             reads=[("vhat", t), "cstb"], writes=[("ps", wb)])
                k = ring("tmp", 4)
                P.op("dve", lambda e: e.scalar_tensor_tensor(
                    out=tmp[k].rearrange("p (t q) -> p t q", q=128),
                    in0=bank(wb).rearrange("p (t q) -> p t q", q=128),
                    scalar=cst[:, C_LNG + h:C_LNG + h + 1],
                    in1=bmat[:, h * 128:(h + 1) * 128].unsqueeze(1).broadcast_to([128, 4, 128]),
                    op0=ALU.mult, op1=ALU.add),
                    reads=[("ps", wb), "cst", "bmat"], writes=[("tmp", k)])
                P.op("dve", lambda e: e.tensor_tensor(out=sT[:, h, :], in0=tmp[k], in1=uT[:, h, :], op=ALU.mult),
                     reads=[("tmp", k), ("uT", h)], writes=[("sT", h)])

            for c in range(4):
                proj_chunk(U0, c, uT, "uT", AF.Gelu_apprx_tanh, 1.0)
            nxt = None
            for h in range(4):
                sgu_group(h)
                for c in range(4):
                    cc = (h % 2) * 4 + c
                    if h < 2:
                        proj_chunk(GA0, cc, taT, "taT", AF.Tanh, 0.5)
                    else:
                        proj_chunk(GB0, cc, tbT, "tbT", AF.Tanh, 0.5)
            if i + 1 < nblk:
                nxt = pre_a("b", tiles_for(i + 1), xt, EPS)
            for dc in range(8):
                yp = ypairs[ring("ypair", 2)]
                mm_group(bank(yp[0]), [(wfo[:, g, dc * 128:(dc + 1) * 128], yb[:, g, :]) for g in range(4)],
                         reads=["wfo", ("Yb", i % 2)], writes=[("ps", yp[0])])
                mm_group(bank(yp[1]), [(wso[:, h, dc * 128:(dc + 1) * 128], sT[:, h, :]) for h in range(4)],
                         reads=["wso"] + [("sT", h) for h in range(4)], writes=[("ps", yp[1])])
                k1 = ring("tmp", 4)
                P.op("dve", lambda e, k1=k1, dc=dc, yp=yp: e.scalar_tensor_tensor(
                    out=tmp[k1], in0=taT[:, dc, :], scalar=1.0, in1=bank(yp[0]), op0=ALU.add, op1=ALU.mult),
                    reads=[("taT", dc), ("ps", yp[0])], writes=[("tmp", k1)])
                k2 = ring("tmp", 4)
                P.op("dve", lambda e, k2=k2, dc=dc, yp=yp: e.scalar_tensor_tensor(
                    out=tmp[k2], in0=tbT[:, dc, :], scalar=1.0, in1=bank(yp[1]), op0=ALU.add, op1=ALU.mult),
                    reads=[("tbT", dc), ("ps", yp[1])], writes=[("tmp", k2)])
                P.op(ADD_ENG, lambda e, k1=k1, k2=k2, dc=dc: e.tensor_tensor(out=mT[:, dc, :], in0=tmp[k1],
                                                                          in1=tmp[k2], op=ALU.add),
                     reads=[("tmp", k1), ("tmp", k2)], writes=[("mT", dc)])
            if nxt is not None:
                pre_b(nxt, xs, (0, 3), gpm)
        b5(nblk - 1)

    def phase_c(src):
        arena.release(base_mark)
        nblk = cfg.get("nblk", NBLK)
        NH = 2 * NBLK
        wd = arena.alloc(32 * 1024, BF16).rearrange("p (j d) -> p j d", d=1024)
        NW = 3
        wup = [arena.alloc(2048, BF16).rearrange("p (kc h f) -> p kc h f", kc=8, h=2) for _ in range(NW)]
        xt = [arena.alloc(D, F32) for _ in range(4)]
        xs = [arena.alloc(D, BF16) for _ in range(2)]
        h2T = [arena.alloc(8 * 512, BF16).rearrange("p (k t) -> p k t", t=512) for _ in range(2)]
        actT = arena.alloc(32 * 512, BF16).rearrange("p (j t) -> p j t", t=512)
        cgb = [arena.alloc(512, F32) for _ in range(3)]
        cvb = [arena.alloc(512, F32) for _ in range(3)]
        ggb = [arena.alloc(512, BF16) for _ in range(3)]
        tt = [arena.alloc(D, F32) for _ in range(2)]
        xr = [arena.alloc(D, F32) for _ in range(2)]
        haloU = arena.alloc(64 * NH, F32).rearrange("p (c t) -> p c t", t=NH)
        hhT = arena.alloc(8 * NH, BF16).rearrange("p (k t) -> p k t", t=NH)
        gpof = cst[:, C_GPOF:C_GPOF + 1024]

        P.op("pool", dma1(wd, w_down.rearrange("(j p) d -> p j d", p=128)), writes=["wd"], ndma=1)

        def wload(sl, j):
            P.op("sp", dma1(wup[sl], wups[j].rearrange("p (kc h f) -> p kc h f", kc=8, h=2)),
                 reads=[("wups", j)], writes=[("wup", sl)], ndma=1)

        srcb = src.rearrange("(i r) d -> i r d", r=BLK)
        pieces = [(0, 1, src[0:1, :]),
                  (1, NBLK, srcb[0:NBLK - 1, BLK - 1, :]),
                  (NBLK, 2 * NBLK - 1, srcb[1:NBLK, 0, :]),
                  (2 * NBLK - 1, 2 * NBLK, src[NTOK - 1:NTOK, :])]
        st = pre_a("c", [(NH, pieces, hhT[:, :, :], "hhT")], xt, EPS, hm_col=cst[:, C_HM:C_HM + 1])
        pre_b(st, xs, 0, gpf)
        def tiles_for(i):
            t0 = i * BLK
            hb = h2T[i % 2]
            res = ("h2T", i % 2)
            return [(128, src[t0 + t * 128:t0 + (t + 1) * 128, :], hb[:, :, t * 128:(t + 1) * 128], res)
                    for t in range(4)]

        st0 = pre_a("c", tiles_for(0), xt, EPS)
        pre_b(st0, xs, 0, gpf)
        hbanks = [6, 7]
        PER = 12
        NHW = 8
        actflat = actT.rearrange("p j t -> p (j t)")
        hw = [actflat[:, sl * 2048:(sl + 1) * 2048].rearrange("p (kc h f) -> p kc h f", kc=8, h=2) for sl in range(NHW)]

        def hload(sl, j):
            P.op("sp", dma1(hw[sl], wups[j].rearrange("p (kc h f) -> p kc h f", kc=8, h=2)),
                 reads=[("wups", j)], writes=[("hw", sl)], ndma=1)
        for j in range(NHW):
            hload(j % NHW, j)
        for c0 in range(0, 64, PER):
            hbk = hbanks[(c0 // PER) % 2]
            cs_ = list(range(c0, min(c0 + PER, 64)))
            for c in cs_:
                j, half = c // 2, c % 2
                sl = j % NHW
                mm_group(bank(hbk)[:, (c - c0) * NH:(c - c0 + 1) * NH],
                         [(hw[sl][:, kc, half, :], hhT[:, kc, :]) for kc in range(8)],
                         reads=[("hw", sl), "hhT"], writes=[("ps", hbk)])
                if half == 1 and j + NHW < 32:
                    hload((j + NHW) % NHW, j + NHW)
            n = len(cs_)
            for half in range(2):
                idx = [ci for ci in cs_ if ci % 2 == half]
                j0 = idx[0] // 2
                srcv = bank(hbk)[:, 0:n * NH].rearrange("p (c t) -> p c t", t=NH)
                first = idx[0] - c0
                evac("dve" if half == 0 else "act",
                     haloU[:, half * 32 + j0:half * 32 + j0 + len(idx), :],
                     srcv[:, first:n:2, :], reads=[("ps", hbk)], writes=["haloU"])

        cw3 = cst[:, C_CW:C_CW + 192].rearrange("p (c t) -> p c t", t=3)
        for side in range(2):
            hv = haloU[:, :, side * NBLK:(side + 1) * NBLK]
            wsd = cw3[:, :, 2 * side:2 * side + 1].broadcast_to([128, 64, NBLK])
            cbb = cst[:, C_CB:C_CB + 64].unsqueeze(2).broadcast_to([128, 64, NBLK])
            P.op("dve", lambda e, hv=hv, wsd=wsd: e.tensor_tensor(out=hv, in0=hv, in1=wsd, op=ALU.mult),
                 reads=["haloU", "cst"], writes=["haloU"])
            P.op("dve", lambda e, hv=hv, cbb=cbb: e.tensor_tensor(out=hv, in0=hv, in1=cbb, op=ALU.add),
                 reads=["haloU", "cst"], writes=["haloU"])

        chunks = [(i, j) for i in range(nblk) for j in range(32)]
        loaded = [0]
        wslot = {}

        def ensure_loaded(upto):
            while loaded[0] <= min(upto, len(chunks) - 1):
                n = loaded[0]
                i, j = chunks[n]
                wslot[n] = n % NW
                wload(n % NW, j)
                loaded[0] += 1

        uppairs = [(1, 2), (3, 4), (5, 6)]
        pdslots = [5, 1, 3]

        pending_fin = [None]
        for i in range(nblk):
            hb = h2T[i % 2]
            hres = ("h2T", i % 2)
            nxt = None
            for j in range(32):
                n = i * 32 + j
                ensure_loaded(n + NW - 1)
                sl = wslot[n]
                w = wup[sl]
                pr = uppairs[ring("uppair", 3)]
                for half in range(2):
                    b = pr[half]
                    mm_group(bank(b), [(w[:, kc, half, :], hb[:, kc, :]) for kc in range(8)],
                             reads=[("wup", sl), hres], writes=[("ps", b)])
                k = ring("cg", 3)
                cg, cv, gg = cgb[k], cvb[k], ggb[k]
                trip = ((0, cg, "cg"), (1, cv, "cv"))
                for half, cbuf, cn in trip:
                    b = pr[half]
                    c = half * 32 + j
                    w1 = cst[:, C_CW + 3 * c + 1:C_CW + 3 * c + 2]
                    bb = cst[:, C_CB + c:C_CB + c + 1]
                    P.op("act", lambda e, cbuf=cbuf, b=b, w1=w1, bb=bb: e.activation(
                        out=cbuf[:, 1:511], in_=bank(b)[:, 1:511], func=AF.Identity, bias=bb, scale=w1),
                        reads=[("ps", b), "cst"], writes=[(cn, k, "m")])
                    for side in range(2):
                        col = 511 * side
                        hv = haloU[:, c, side * NBLK + i:side * NBLK + i + 1]
                        P.op("act", lambda e, cbuf=cbuf, b=b, w1=w1, hv=hv, col=col: e.activation(
                            out=cbuf[:, col:col + 1], in_=bank(b)[:, col:col + 1], func=AF.Identity, bias=hv,
                            scale=w1),
                            reads=[("ps", b), "cst", "haloU"], writes=[(cn, k, side)])
                for side in range(2):
                    for half, cbuf, cn in trip:
                        b = pr[half]
                        c = half * 32 + j
                        wt = cst[:, C_CW + 3 * c + 2 * side:C_CW + 3 * c + 2 * side + 1]
                        pb = bank(b)
                        o_ = cbuf[:, 1:512] if side == 0 else cbuf[:, 0:511]
                        i_ = pb[:, 0:511] if side == 0 else pb[:, 1:512]
                        rr = [(cn, k, "m"), (cn, k, 1 - side)]
                        P.op("dve", lambda e, o_=o_, i_=i_, wt=wt: e.scalar_tensor_tensor(
                            out=o_, in0=i_, scalar=wt, in1=o_, op0=ALU.mult, op1=ALU.add),
                            reads=[("ps", b), "cst"] + rr, writes=rr)
                def finish(k=k, cg=cg, cv=cv, gg=gg, j=j):
                    P.op("act", lambda e: e.activation(out=gg, in_=cg, func=AF.Gelu_apprx_tanh),
                         reads=[("cg", k, "m"), ("cg", k, 0), ("cg", k, 1)], writes=[("gg", k)])
                    P.op(MUL_ENG, lambda e: e.tensor_tensor(out=actT[:, j, :], in0=gg, in1=cv, op=ALU.mult),
                         reads=[("gg", k), ("cv", k, "m"), ("cv", k, 0), ("cv", k, 1)], writes=[("act", j)])
                if pending_fin[0] is not None:
                    pending_fin[0]()
                pending_fin[0] = finish
            pending_fin[0]()
            pending_fin[0] = None
            t0 = i * BLK
            if i + 1 < nblk:
                nxt = pre_a("c", tiles_for(i + 1), xt, EPS)
            for t in range(4):
                b0 = pdslots[ring("pd", 3)]
                k2 = ring("xr", 2)
                rows = slice(t0 + t * 128, t0 + (t + 1) * 128)
                P.op("sp", dma1(xr[k2], src[rows, :]), writes=[("xr", k2)], ndma=1)
                if t == 0:
                    cuts = [0, 22, 26, 28, 30, 31, 32]
                    for ci in range(len(cuts) - 1):
                        j0, j1 = cuts[ci], cuts[ci + 1]
                        for dh in range(2):
                            P.op("pe", lambda e, t=t, dh=dh, b0=b0, j0=j0, j1=j1: [e.matmul(
                                bank(b0 + dh), lhsT=actT[:, j, t * 128:(t + 1) * 128],
                                rhs=wd[:, j, dh * 512:(dh + 1) * 512], start=(j == 0), stop=(j == 31))
                                for j in range(j0, j1)][-1],
                                reads=[("act", j) for j in range(j0, j1)] + ["wd"], writes=[("ps", b0 + dh)])
                else:
                    for dh in range(2):
                        mm_group(bank(b0 + dh),
                                 [(actT[:, j, t * 128:(t + 1) * 128], wd[:, j, dh * 512:(dh + 1) * 512])
                                  for j in range(32)],
                                 reads=[("act", j) for j in range(32)] + ["wd"], writes=[("ps", b0 + dh)])
                if t == 1 and nxt is not None:
                    pre_b(nxt, xs, 0, gpf)
                s2 = ring("ss2", 4)
                pd = bank(b0, 2)
                jk = ring("sqjunk", 2)
                P.op("act", lambda e, pd=pd, s2=s2, jk=jk: e.activation(out=sqjunks[jk], in_=pd, func=AF.Square,
                                                                       accum_out=ssq2[s2][:, 0:1]),
                     reads=[("ps", b0), ("ps", b0 + 1)], writes=[("sqjunk", jk), ("ssq2", s2)])
                P.op("pool", lambda e, s2=s2: e.tensor_scalar(out=rsd2[s2][:, 0:1], in0=ssq2[s2][:, 0:1],
                                                             scalar1=1.0 / D, scalar2=EPS, op0=ALU.mult,
                                                             op1=ALU.add),
                     reads=[("ssq2", s2)], writes=[("rsd2", s2)])
                P.op("pool", lambda e, s2=s2: e.tensor_tensor(out=rsd2[s2][:, 0:1], in0=rsd2[s2][:, 0:1],
                                                             in1=neghalf[:, 0:1], op=ALU.pow),
                     reads=[("rsd2", s2), "neghalf"], writes=[("rsd2", s2)])
                k = ring("tt", 2)
                P.op("dve", lambda e, pd=pd, s2=s2, k=k: e.scalar_tensor_tensor(
                    out=tt[k], in0=pd, scalar=rsd2[s2][:, 0:1], in1=gpof, op0=ALU.mult, op1=ALU.mult),
                    reads=[("ps", b0), ("ps", b0 + 1), ("rsd2", s2), "cst"], writes=[("tt", k)])
                P.op(ADD_ENG, lambda e, k=k, k2=k2: e.tensor_tensor(out=tt[k], in0=tt[k], in1=xr[k2], op=ALU.add),
                     reads=[("tt", k), ("xr", k2)], writes=[("tt", k)])
                P.op("sp", dma1(y[rows, :], tt[k]), reads=[("tt", k)], writes=[("y", i, t)], ndma=1)

    if "A" in phases:
        phase_a()
        P.barrier()
    if "B" in phases:
        phase_b()
        P.barrier()
    if "C" in phases:
        if "A" not in phases:
            cast_wup()
        phase_c(xn if cfg.get("c_src") == "xn" else x1s)
    P.emit()
    return nc


def _core_kind(c):
    return "sample" if c < 2 else "prompt"


def _core_tokens(c, x_prompt, x_sample):
    if c < 2:
        return np.concatenate([x_sample[c], x_prompt[c]], axis=0)
    b0 = 2 + 5 * (c - 2)
    return x_prompt[b0:b0 + 5].reshape(NTOK, D)


_TABLES = {}


def _tables(kind):
    if kind not in _TABLES:
        _TABLES[kind] = _dft_tables(kind)
    return _TABLES[kind]


def make_in_maps(inputs, cores):
    f = lambda a: np.ascontiguousarray(np.asarray(a, dtype=np.float32))
    x_prompt = f(inputs["x_prompt"])
    x_sample = f(inputs["x_sample"])
    w_in = f(inputs["w_in"][0])
    shared = {
        "w_in": w_in,
        "w_fo": f(inputs["w_fourier_out"][0]),
        "w_so": f(inputs["w_sgu_out"][0]),
        "w_o": f(inputs["w_o"][0]),
        "w_up": f(inputs["w_up"][0]),
        "w_down": f(inputs["w_down"][0]),
    }
    cst0 = np.zeros((128, NCST), dtype=np.float32)
    col = lambda v: f(v).reshape(8, 128).T
    cst0[:, C_GPM:C_GPM + 8] = col(inputs["norm_pre_mix"][0])
    cst0[:, C_GPF:C_GPF + 8] = col(inputs["norm_pre_ffn"][0])
    cst0[:, C_GPOM:C_GPOM + 1024] = f(inputs["norm_post_mix"][0])[None, :]
    cst0[:, C_GPOF:C_GPOF + 1024] = f(inputs["norm_post_ffn"][0])[None, :]
    cw = f(inputs["conv_w"][0])
    cst0[:, C_CW:C_CW + 192] = cw.reshape(3, 64, 128).transpose(2, 1, 0).reshape(128, 192)
    cst0[:, C_CB:C_CB + 64] = f(inputs["conv_b"][0]).reshape(64, 128).T
    cst0[:, C_LNG:C_LNG + 4] = f(inputs["sgu_ln_g"][0]).reshape(4, 128).T
    cst0[:, C_LNB:C_LNB + 4] = f(inputs["sgu_ln_b"][0]).reshape(4, 128).T
    cst0[:, C_BS:C_BS + 512] = f(inputs["sgu_b_s"][0]).reshape(1, 512)
    wst = f(inputs["sgu_w_s"][0]).transpose(2, 0, 1).reshape(128, 512)
    cst0[:, C_WST:C_WST + 512] = wst
    maps = []
    for c in cores:
        kind = _core_kind(c)
        t8, t2, cs, m8, m2 = _tables(kind)
        xn = np.ascontiguousarray(_core_tokens(c, x_prompt, x_sample))
        xfo = np.ascontiguousarray(xn[_fourier_row_order(kind)])
        cst = cst0.copy()
        cst[:, C_HM:C_HM + 1] = _halo_mask(kind)
        cstb = np.zeros((128, NCSTB), dtype=np.float32)
        cstb[:, B_ID:B_ID + 128] = np.eye(128, dtype=np.float32)
        cstb[:, B_CS:B_CS + 512] = cs
        cstb[:, B_M8:B_M8 + 256] = m8
        cstb[:, B_M2:B_M2 + 256] = m2
        cstb[:, B_WST:B_WST + 512] = wst
        m = dict(shared)
        m.update({"xn": xn, "xf": xfo, "cst": cst, "cstb": cstb, "t8": t8, "t2": t2})
        maps.append(m)
    return maps


def assemble_outputs(ys):
    y_prompt = np.empty((32, 2048, D), dtype=np.float32)
    y_sample = np.empty((2, 8192, D), dtype=np.float32)
    for c in range(NCORES):
        yc = ys[c]
        if c < 2:
            y_sample[c] = yc[:S8]
            y_prompt[c] = yc[S8:]
        else:
            b0 = 2 + 5 * (c - 2)
            y_prompt[b0:b0 + 5] = yc.reshape(5, 2048, D)
    return y_prompt, y_sample


def kernel(**inputs):
    nc = build_program()
    maps = make_in_maps(inputs, list(range(NCORES)))
    res = run_bass_kernel_spmd(nc, maps, core_ids=list(range(NCORES)))
    ys = [np.asarray(r["y"], dtype=np.float32) for r in res.results]
    return assemble_outputs(ys)
```

```python
import numpy as np
import concourse.bass as bass
import concourse.mybir as mybir
from concourse.bass_utils import run_bass_kernel_spmd

_BF16NP = mybir.dt.np(mybir.dt.bfloat16)

F32 = mybir.dt.float32
BF16 = mybir.dt.bfloat16
AF = mybir.ActivationFunctionType
ALU = mybir.AluOpType

D = 1024
NTOK = 10240
S8 = 8192
S2 = 2048
BLK = 512
NBLK = NTOK // BLK
DFF = 4096
EPS = 1e-6
NCORES = 8

ENGS = ("pe", "act", "dve", "pool", "sp")
NDMA_SEMS = 44
NDMA_HW = 32


class Op:
    __slots__ = ("eng", "fn", "deps", "signal", "count", "is_dma", "ndma", "idx", "eidx", "sem", "semval")


class Prog:
    def __init__(self, nc):
        self.nc = nc
        self.ops = []
        self.eng_ops = {e: [] for e in ENGS}
        self.last_writer = {}
        self.readers = {}
        self.seen = {e: {} for e in ENGS}
        self.seen_dma = {e: set() for e in ENGS}
        self.dma_last = [None] * NDMA_SEMS
        self.dma_use = [0] * NDMA_SEMS
        self.dma_rr = 0
        self.dma_rr_sw = 0

    def op(self, eng, fn, reads=(), writes=(), ndma=0):
        o = Op()
        o.eng = eng
        o.fn = fn
        o.signal = False
        o.count = 0
        o.is_dma = ndma > 0
        o.ndma = ndma
        o.idx = len(self.ops)
        o.eidx = len(self.eng_ops[eng])
        o.sem = None
        o.semval = 0
        deps = {}
        for r in reads:
            w = self.last_writer.get(r)
            if w is not None and not (w.eng == "pe" and eng == "pe" and not w.is_dma):
                deps[w.idx] = w
            self.readers.setdefault(r, []).append(o)
        for r in writes:
            w = self.last_writer.get(r)
            if w is not None and w is not o and not (w.eng == "pe" and eng == "pe" and not w.is_dma):
                deps[w.idx] = w
            for rd in self.readers.get(r, ()):
                if rd is not o:
                    deps[rd.idx] = rd
            self.last_writer[r] = o
            self.readers[r] = []
        if o.is_dma:
            if eng == "pool":
                s = NDMA_HW + self.dma_rr_sw
                self.dma_rr_sw = (self.dma_rr_sw + 1) % (NDMA_SEMS - NDMA_HW)
            else:
                s = self.dma_rr
                self.dma_rr = (self.dma_rr + 1) % NDMA_HW
            prev = self.dma_last[s]
            if prev is not None:
                deps[prev.idx] = prev
            self.dma_use[s] += 16 * ndma
            o.sem = s
            o.semval = self.dma_use[s]
            self.dma_last[s] = o
        keep = []
        best = {}
        for d in deps.values():
            if d.is_dma:
                if d.idx in self.seen_dma[eng]:
                    continue
                self.seen_dma[eng].add(d.idx)
                keep.append(d)
            else:
                if self.seen[eng].get(d.eng, -1) >= d.eidx:
                    continue
                if d.eng not in best or best[d.eng].eidx < d.eidx:
                    best[d.eng] = d
        for e, d in best.items():
            self.seen[eng][e] = d.eidx
            keep.append(d)
        for d in keep:
            d.signal = True
        o.deps = keep
        self.ops.append(o)
        self.eng_ops[eng].append(o)
        return o

    def barrier(self):
        dmas = [d for d in self.dma_last if d is not None]
        for e in ENGS:
            o = Op()
            o.eng = e
            o.fn = None
            o.signal = False
            o.count = 0
            o.is_dma = False
            o.ndma = 0
            o.idx = len(self.ops)
            o.eidx = len(self.eng_ops[e])
            o.sem = None
            o.semval = 0
            keep = []
            for e2 in ENGS:
                lst = self.eng_ops[e2]
                k = len(lst) - 1
                while k >= 0 and (lst[k].is_dma or lst[k].fn is None):
                    k -= 1
                if k >= 0 and e2 != e:
                    d = lst[k]
                    if self.seen[e].get(e2, -1) < d.eidx:
                        self.seen[e][e2] = d.eidx
                        keep.append(d)
            for d in dmas:
                if d.idx not in self.seen_dma[e]:
                    self.seen_dma[e].add(d.idx)
                    keep.append(d)
            for d in keep:
                d.signal = True
            o.deps = keep
            self.ops.append(o)
            self.eng_ops[e].append(o)
        self.last_writer = {}
        self.readers = {}

    def emit(self, final_wait_engine="sp"):
        nc = self.nc
        tail = [d for d in self.dma_last if d is not None]
        import contextlib
        with contextlib.ExitStack() as es:
            eng_sem = {e: es.enter_context(nc.semaphore("sem_" + e)) for e in ENGS}
            dma_sems = [es.enter_context(nc.semaphore("dsem%d" % i)) for i in range(NDMA_SEMS)]
            for e in ENGS:
                c = 0
                for o in self.eng_ops[e]:
                    if o.signal and not o.is_dma:
                        c += 1
                        o.count = c
            block = es.enter_context(nc.Block())

            def run(ename, eng):
                for o in self.eng_ops[ename]:
                    for d in o.deps:
                        if d.is_dma:
                            eng.wait_ge(dma_sems[d.sem], d.semval)
                        else:
                            eng.wait_ge(eng_sem[d.eng], d.count)
                    if o.fn is None:
                        assert not o.signal
                        continue
                    if o.is_dma:
                        o.fn(eng, dma_sems[o.sem])
                    else:
                        ins = o.fn(eng)
                        if o.signal:
                            ins.then_inc(eng_sem[ename], 1)
                if ename == final_wait_engine:
                    for d in tail:
                        eng.wait_ge(dma_sems[d.sem], d.semval)

            block.tensor(lambda eng: run("pe", eng))
            block.scalar(lambda eng: run("act", eng))
            block.vector(lambda eng: run("dve", eng))
            block.gpsimd(lambda eng: run("pool", eng))
            block.sync(lambda eng: run("sp", eng))


class Arena:
    def __init__(self, nc, nbytes):
        self.t = nc.alloc_sbuf_tensor("arena", [128, nbytes // 4], F32)
        self.nbytes = nbytes
        self.off = 0

    def mark(self):
        return self.off

    def release(self, m):
        self.off = m

    def alloc(self, n, dtype):
        nb = n * (2 if dtype == BF16 else 4)
        nb = (nb + 31) // 32 * 32
        assert self.off + nb <= self.nbytes, ("arena overflow", self.off, nb, self.nbytes)
        ap = self.t[:, self.off // 4:(self.off + nb) // 4]
        self.off += nb
        if dtype == BF16:
            ap = ap.bitcast(BF16)
        return ap[:, 0:n]


C_GPM, C_GPF = 0, 8
C_GPOM = 16
C_GPOF = C_GPOM + 1024
C_CW = C_GPOF + 1024
C_CB = C_CW + 192
C_LNG = C_CB + 64
C_LNB = C_LNG + 4
C_BS = C_LNB + 4
C_WST = C_BS + 512
C_HM = C_WST + 512
NCST = C_HM + 1
NCST = (NCST + 7) // 8 * 8
B_ID = 0
B_CS = 128
B_M8 = B_CS + 512
B_M2 = B_M8 + 256
B_WST = B_M2 + 256
NCSTB = B_WST + 512


def _dft_tables(kind):
    rho = np.arange(128)[:, None, None].astype(np.float64)
    kap = np.arange(128)[None, None, :].astype(np.float64)
    tau = np.arange(64)[None, :, None].astype(np.float64)
    if kind == "sample":
        psi = 2 * np.pi * (rho * kap / 128.0 + tau * kap / 8192.0)
    else:
        psi = 2 * np.pi * (rho * kap / 128.0 + (tau % 16) * kap / 2048.0)
    sc = 1.0 / np.sqrt(128.0)
    t8 = np.concatenate([np.cos(psi) * sc, -np.sin(psi) * sc], axis=2)
    t = np.arange(64)
    if kind == "sample":
        th = 2 * np.pi * np.outer(t, t) / 64.0
        mc = np.cos(th) / 8.0
        ms = np.sin(th) / 8.0
    else:
        same = (t[:, None] // 16) == (t[None, :] // 16)
        th = 2 * np.pi * np.outer(t % 16, t % 16) / 16.0
        mc = np.where(same, np.cos(th), 0.0) / 4.0
        ms = np.where(same, np.sin(th), 0.0) / 4.0
    eye2 = np.eye(2)
    m8c = np.kron(mc, eye2)
    m8s = np.kron(ms, eye2)
    tau2 = np.arange(16)[None, :, None].astype(np.float64)
    psi2 = 2 * np.pi * (rho * kap / 128.0 + tau2 * kap / 2048.0)
    t2 = np.concatenate([np.cos(psi2) * sc, -np.sin(psi2) * sc], axis=2)
    t16 = np.arange(16)
    th2 = 2 * np.pi * np.outer(t16, t16) / 16.0
    eye8 = np.eye(8)
    m2c = np.kron(np.cos(th2) / 4.0, eye8)
    m2s = np.kron(np.sin(th2) / 4.0, eye8)
    c = np.arange(128)
    ph = 2 * np.pi * np.outer(c, c) / 128.0
    C = np.cos(ph) * sc
    S = np.sin(ph) * sc
    cs = np.concatenate([C, -S, S, C], axis=1)
    return (t8.reshape(128, 64 * 256).astype(_BF16NP), t2.reshape(128, 16 * 256).astype(_BF16NP),
            cs.astype(np.float32), np.concatenate([m8c, m8s], axis=1).astype(np.float32),
            np.concatenate([m2c, m2s], axis=1).astype(np.float32))


def _fourier_row_order(kind):
    idx = np.empty(NTOK, dtype=np.int64)
    tau = np.arange(64)[:, None]
    rho = np.arange(128)[None, :]
    if kind == "sample":
        tok = tau + 64 * rho
    else:
        tok = 2048 * (tau // 16) + (tau % 16) + 16 * rho
    idx[:S8] = tok.reshape(-1)
    tau2 = np.arange(16)[:, None]
    idx[S8:] = (S8 + tau2 + 16 * rho).reshape(-1)
    return idx


def _halo_mask(kind):
    hm = np.zeros((128, 1), dtype=np.float32)
    if kind == "sample":
        bounds = [0, S8, NTOK]
    else:
        bounds = list(range(0, NTOK + 1, 2048))
    for i in range(NBLK):
        t0 = i * BLK
        hm[i, 0] = 0.0 if t0 in bounds else 1.0
        hm[NBLK + i, 0] = 0.0 if (t0 + BLK) in bounds else 1.0
    return hm


def build_program(cfg=None):
    cfg = cfg or {}
    MUL_ENG = cfg.get("mul_eng", "pool")
    ADD_ENG = cfg.get("add_eng", "pool")
    phases = cfg.get("phases", "ABC")
    dbg = cfg.get("debug", False)
    nc = bass.Bass("TRN2", target_bir_lowering=False)

    def din(name, shape):
        return nc.dram_tensor(name, shape, F32, kind="ExternalInput").ap()

    xn = din("xn", [NTOK, D])
    xf = din("xf", [NTOK, D])
    w_in = din("w_in", [D, 3584])
    w_fo = din("w_fo", [512, D])
    w_so = din("w_so", [512, D])
    w_o = din("w_o", [D, D])
    w_up = din("w_up", [D, 2 * DFF])
    w_down = din("w_down", [DFF, D])
    cst_d = din("cst", [128, NCST])
    cstb_d = din("cstb", [128, NCSTB])
    t8_d = nc.dram_tensor("t8", [128, 64 * 256], BF16, kind="ExternalInput").ap()
    t2_d = nc.dram_tensor("t2", [128, 16 * 256], BF16, kind="ExternalInput").ap()
    y = nc.dram_tensor("y", [NTOK, D], F32, kind="ExternalOutput").ap()
    kscr = "ExternalOutput" if dbg else "Internal"
    yscr = nc.dram_tensor("yscr", [4, 128, NTOK], BF16, kind=kscr).ap()
    x1s = nc.dram_tensor("x1s", [NTOK, D], F32, kind=kscr).ap()
    wups = nc.dram_tensor("wups", [32, 128, 2048], BF16, kind="Internal").ap()

    P = Prog(nc)
    arena = Arena(nc, 212000)
    psum = nc.alloc_psum_tensor("psum", [128, 4096], F32)

    def bank(b, n=1):
        return psum[:, 512 * b:512 * (b + n)]

    def bankb(b):
        return psum[:, 512 * b:512 * (b + 1)].bitcast(BF16)

    cst = arena.alloc(NCST, F32)
    cstb = arena.alloc(NCSTB, BF16)
    neghalf = arena.alloc(8, F32)
    ssq = [arena.alloc(8, F32) for _ in range(4)]
    rsd = [arena.alloc(8, F32) for _ in range(4)]
    ssq2 = [arena.alloc(8, F32) for _ in range(4)]
    rsd2 = [arena.alloc(8, F32) for _ in range(4)]
    ident = cstb[:, B_ID:B_ID + 128]
    gpm = cst[:, C_GPM:C_GPM + 8]
    gpf = cst[:, C_GPF:C_GPF + 8]

    def dma1(out, in_, **kw):
        def f(eng, sem):
            eng.dma_start(out=out, in_=in_, **kw).then_inc(sem, 16)
        return f

    P.op("sp", dma1(cst, cst_d), writes=["cst"], ndma=1)
    P.op("pool", dma1(cstb, cstb_d), writes=["cstb"], ndma=1)
    P.op("pool", lambda e: e.memset(neghalf, -0.5), writes=["neghalf"])
    for i in range(4):
        P.op("pool", lambda e, i=i: e.memset(ssq[i], 1.0), writes=[("ssq", i, t) for t in range(8)])
    sqjunks = [arena.alloc(D, BF16) for _ in range(2)]

    def cast_wup():
        wv = w_up.rearrange("(kc p) (h j f) -> j p kc h f", p=128, h=2, f=128)
        for j in range(32):
            ov = wups[j].rearrange("p (kc h f) -> p kc h f", kc=8, h=2)

            def f(eng, sem, j=j, ov=ov):
                for h in range(2):
                    eng.dma_start(out=ov[:, :, h, :], in_=wv[j][:, :, h, :]).then_inc(sem, 16)
            P.op("pool", f, writes=[("wups", j)], ndma=2)

    ring_ctr = {}

    def ring(name, n):
        c = ring_ctr.get(name, 0)
        ring_ctr[name] = c + 1
        return c % n

    def pre_a(tag, tiles, xt_bufs, eps, hm_col=None):
        assert len(xt_bufs) >= len(tiles)
        s = ring(tag + "ss", 4)
        ss, rs = ssq[s], rsd[s]
        st = []
        for ti, (Pn, src, dst, dres) in enumerate(tiles):
            k = ring(tag + "xt", len(xt_bufs))
            xt = xt_bufs[k]
            st.append((Pn, src, dst, dres, k))
            pieces = src if isinstance(src, list) else [(0, Pn, src)]

            def ld(eng, sem, xt=xt, pieces=pieces):
                for (p0, p1, ap) in pieces:
                    eng.dma_start(out=xt[p0:p1, :], in_=ap).then_inc(sem, 16)
            P.op("sp", ld, writes=[(tag + "xt", k)], ndma=len(pieces))
        for ti, (Pn, src, dst, dres, k) in enumerate(st):
            xt = xt_bufs[k]
            jk = ring("sqjunk", 2)
            P.op("act", lambda e, xt=xt, Pn=Pn, ti=ti, ss=ss, jk=jk: e.activation(
                out=sqjunks[jk][:Pn, :], in_=xt[:Pn, :], func=AF.Square, accum_out=ss[:Pn, ti:ti + 1]),
                reads=[(tag + "xt", k)], writes=[("sqjunk", jk), ("ssq", s, ti)])
        nt = len(tiles)
        P.op("pool", lambda e: e.tensor_scalar(out=rs[:, 0:nt], in0=ss[:, 0:nt], scalar1=1.0 / D, scalar2=eps,
                                               op0=ALU.mult, op1=ALU.add),
             reads=[("ssq", s, ti) for ti in range(nt)], writes=[("rsd", s)])
        P.op("pool", lambda e: e.tensor_tensor(out=rs[:, 0:nt], in0=rs[:, 0:nt], in1=neghalf[:, 0:nt], op=ALU.pow),
             reads=[("rsd", s), "neghalf"], writes=[("rsd", s)])
        if hm_col is not None:
            P.op("pool", lambda e: e.tensor_tensor(out=rs[:, nt - 1:nt], in0=rs[:, nt - 1:nt], in1=hm_col,
                                                   op=ALU.mult),
                 reads=[("rsd", s), "cst"], writes=[("rsd", s)])
        return (tag, st, s, xt_bufs)

    def pre_b(state, xs_bufs, tbank, gvec):
        tag, st, s, xt_bufs = state
        rs = rsd[s]
        for ti, (Pn, src, dst, dres, k) in enumerate(st):
            xt = xt_bufs[k]
            k2 = ring(tag + "xs", len(xs_bufs))
            xs = xs_bufs[k2]
            P.op("act", lambda e, xt=xt, xs=xs, Pn=Pn, ti=ti: e.activation(
                out=xs[:Pn, :], in_=xt[:Pn, :], func=AF.Copy, scale=rs[:Pn, ti:ti + 1]),
                reads=[(tag + "xt", k), ("rsd", s)], writes=[(tag + "xs", k2)])
            tbk = tbank[ti % len(tbank)] if isinstance(tbank, tuple) else tbank
            tb = bankb(tbk)

            def tr(e, xs=xs, Pn=Pn, tb=tb):
                ins = None
                for kc in range(8):
                    ins = e.transpose(out=tb[:, kc * 128:kc * 128 + Pn], in_=xs[:Pn, kc * 128:(kc + 1) * 128],
                                      identity=ident[:Pn, :Pn])
                return ins
            P.op("pe", tr, reads=[(tag + "xs", k2), "cstb"], writes=[("ps", tbk)])
            tb3 = tb.rearrange("p (k t) -> p k t", t=128)
            P.op("dve", lambda e, dst=dst, tb3=tb3, Pn=Pn: e.tensor_tensor(
                out=dst, in0=tb3[:, :, 0:Pn], in1=gvec.unsqueeze(2).broadcast_to([128, 8, Pn]), op=ALU.mult),
                reads=[("ps", tbk), "cst"], writes=[dres])

    def mm_group(out, pairs, reads, writes):
        def f(e):
            ins = None
            n = len(pairs)
            for i, (l, r) in enumerate(pairs):
                ins = e.matmul(out, lhsT=l, rhs=r, start=(i == 0), stop=(i == n - 1))
            return ins
        return P.op("pe", f, reads=reads, writes=writes)

    def evac(eng, out, in_, reads, writes):
        if eng == "act":
            return P.op("act", lambda e: e.activation(out=out, in_=in_, func=AF.Copy), reads=reads, writes=writes)
        return P.op("dve", lambda e: e.tensor_copy(out=out, in_=in_), reads=reads, writes=writes)

    base_mark = arena.mark()

    def both(name):
        return [(name, "act"), (name, "dve")]

    alt = [0]

    def alt_eng():
        alt[0] ^= 1
        return "act" if alt[0] else "dve"

    def phase_a():
        arena.release(base_mark)
        tcs = arena.alloc(64 * 256, BF16)
        Fb = arena.alloc(64 * 512, BF16).rearrange("p (t c) -> p t c", c=512)
        Gb0 = arena.alloc(2 * 64 * 128, BF16)
        YT = arena.alloc(S8, BF16)
        Bb = [arena.alloc(256, BF16) for _ in range(4)]
        ovl_mark = arena.mark()
        wf = arena.alloc(8 * 512, BF16).rearrange("p (k c) -> p k c", c=512)
        xt = [arena.alloc(D, F32) for _ in range(4)]
        xs = [arena.alloc(D, BF16) for _ in range(2)]
        hfT = [arena.alloc(8 * 256, BF16).rearrange("p (k t) -> p k t", t=256) for _ in range(2)]
        end_mark = arena.mark()
        arena.release(ovl_mark)
        Gb1 = arena.alloc(2 * 64 * 128, BF16)
        assert arena.mark() <= end_mark
        arena.release(end_mark)
        ovl_res = ["wf"] + [("axt", k) for k in range(4)] + [("axs", k) for k in range(2)] + \
                  [("hfT", k) for k in range(2)]
        cs1 = cstb[:, B_CS:B_CS + 256]
        cs2 = cstb[:, B_CS + 256:B_CS + 512]

        P.op("pool", dma1(wf, w_in[:, 0:512].rearrange("(k p) c -> p k c", p=128)), writes=["wf"], ndma=1)

        for (tok0, NT1, t_d, mcol, Gbufs) in ((S8, 16, t2_d, B_M2, [Gb0]), (0, 64, t8_d, B_M8, [Gb0, Gb1])):
            E = 128 // NT1
            NB = NT1
            ncol = NT1 * 256
            P.op("sp", dma1(tcs[:, 0:ncol], t_d), writes=["tcs"], ndma=1)
            tc3 = tcs[:, 0:ncol].rearrange("p (t k) -> p t k", k=256)
            mc = cstb[:, mcol:mcol + 128]
            ms = cstb[:, mcol + 128:mcol + 256]
            G5s = [G[:, 0:2 * NB * 128].rearrange("p (a b t e) -> p a b t e", a=2, b=NB, t=NT1, e=E) for G in Gbufs]
            G3s = [G[:, 0:2 * NB * 128].rearrange("p (a b m) -> p a b m", a=2, b=NB) for G in Gbufs]
            YT4 = YT[:, 0:NT1 * 128].rearrange("p (k b e) -> p k b e", b=NB, e=E)
            ng = len(Gbufs)

            def tiles_for(pi):
                hb = hfT[pi % 2]
                res = ("hfT", pi % 2)
                return [(128, xf[tok0 + (2 * pi + u) * 128:tok0 + (2 * pi + u + 1) * 128, :],
                         hb[:, :, u * 128:(u + 1) * 128], res) for u in range(2)]
            npair = NT1 // 2
            sts = {0: pre_a("a", tiles_for(0), xt, EPS)}
            pre_b(sts[0], xs, (0, 7), gpm)
            if npair > 1:
                sts[1] = pre_a("a", tiles_for(1), xt, EPS)
            for pi in range(npair):
                if pi + 1 < npair:
                    pre_b(sts[pi + 1], xs, (0, 7), gpm)
                if pi + 2 < npair:
                    sts[pi + 2] = pre_a("a", tiles_for(pi + 2), xt, EPS)
                hb = hfT[pi % 2]
                for u in range(2):
                    tau = 2 * pi + u
                    fb = (1, 2)[ring("fbank", 2)]
                    mm_group(bank(fb), [(hb[:, kc, u * 128:(u + 1) * 128], wf[:, kc, :]) for kc in range(8)],
                             reads=[("hfT", pi % 2), "wf"], writes=[("ps", fb)])
                    evac("dve", Fb[:, tau, :], bank(fb), reads=[("ps", fb)], writes=[("F", "dve")])

            if tok0 == 0 and "C" in phases:
                cast_wup()

            def stage1(g, tau):
                gi = g % ng
                sb = (3, 4, 2)[ring("s1bank", 3)]
                mm_group(bank(sb)[:, 0:256], [(Fb[:, tau, g * 128:(g + 1) * 128], tc3[:, tau, :])],
                         reads=[("F", "dve"), "tcs"], writes=[("ps", sb)])
                srcv = bank(sb)[:, 0:256].rearrange("p (a b e) -> p a b e", a=2, b=NB, e=E)
                extra = ovl_res if (gi == 1) else []
                evac("dve", G5s[gi][:, :, :, tau, :], srcv, reads=[("ps", sb)], writes=[("G", gi)] + extra)

            state = {"pend": None, "s3cur": None}

            def stage23(g, b):
                gi = g % ng
                if b < NB:
                    s2b = (5, 6)[ring("s2bank", 2)]
                    mm_group(bank(s2b)[:, 0:256], [(G3s[gi][:, 0, b, :], cs1), (G3s[gi][:, 1, b, :], cs2)],
                             reads=[("G", gi), "cstb"], writes=[("ps", s2b)])
                    k = ring("Bb", 4)
                    evac("act", Bb[k], bank(s2b)[:, 0:256], reads=[("ps", s2b)], writes=[("Bb", k)])
                if state["pend"] is not None:
                    pb, pk = state["pend"]
                    q4 = pb % 4
                    if q4 == 0:
                        state["s3cur"] = (7, 1)[ring("s3bank", 2)]
                    s3cur = state["s3cur"]
                    mm_group(bank(s3cur)[:, q4 * 128:(q4 + 1) * 128],
                             [(Bb[pk][:, 0:128], mc), (Bb[pk][:, 128:256], ms)],
                             reads=[("Bb", pk), "cstb"], writes=[("ps", s3cur)])
                    if q4 == 3:
                        b4 = pb // 4
                        srcv = bank(s3cur).rearrange("p (b k e) -> p b k e", b=4, k=NT1, e=E)
                        dstv = YT4[:, :, 4 * b4:4 * b4 + 4, :].rearrange("p k b e -> p b k e")
                        evac("dve", dstv, srcv, reads=[("ps", s3cur)], writes=[("YT", "dve")])
                state["pend"] = (b, k) if b < NB else None

            if ng == 1:
                for g in range(4):
                    for tau in range(NT1):
                        stage1(g, tau)
                    for b in range(NB + 1):
                        stage23(g, b)
                    P.op("sp", dma1(yscr[g][:, tok0:tok0 + NT1 * 128], YT[:, 0:NT1 * 128]),
                         reads=[("YT", "dve")], writes=[("yscr", g, tok0)], ndma=1)
            else:
                for tau in range(NT1):
                    stage1(0, tau)
                for g in range(4):
                    for b in range(NB + 1):
                        stage23(g, b)
                        if g + 1 < 4 and b < NT1:
                            stage1(g + 1, b)
                    P.op("sp", dma1(yscr[g][:, tok0:tok0 + NT1 * 128], YT[:, 0:NT1 * 128]),
                         reads=[("YT", "dve")], writes=[("yscr", g, tok0)], ndma=1)

    def phase_b():
        arena.release(base_mark)
        nblk = cfg.get("nblk_b", NBLK)
        win = arena.alloc(8 * 3072, BF16).rearrange("p (k c) -> p k c", c=3072)
        wfo = arena.alloc(4 * 1024, BF16).rearrange("p (g d) -> p g d", d=1024)
        wso = arena.alloc(4 * 1024, BF16).rearrange("p (g d) -> p g d", d=1024)
        wo = arena.alloc(8 * 1024, BF16).rearrange("p (k d) -> p k d", d=1024)
        bmat = arena.alloc(512, F32)
        ones = arena.alloc(128, BF16)
        xt = [arena.alloc(D, F32) for _ in range(4)]
        xs = [arena.alloc(D, BF16) for _ in range(2)]
        hT = arena.alloc(8 * 512, BF16).rearrange("p (k t) -> p k t", t=512)
        uT = arena.alloc(4 * 512, BF16).rearrange("p (c t) -> p c t", t=512)
        taT = arena.alloc(8 * 512, BF16).rearrange("p (c t) -> p c t", t=512)
        tbT = arena.alloc(8 * 512, BF16).rearrange("p (c t) -> p c t", t=512)
        vg = [arena.alloc(512, F32) for _ in range(2)]
        vhat = arena.alloc(4 * 512, BF16).rearrange("p (t c) -> p t c", c=512)
        st6 = [arena.alloc(8, F32) for _ in range(2)]
        mv = [arena.alloc(8, F32) for _ in range(2)]
        Yb = [arena.alloc(4 * 512, BF16).rearrange("p (g t) -> p g t", t=512) for _ in range(2)]
        sT = arena.alloc(4 * 512, BF16).rearrange("p (h t) -> p h t", t=512)
        mT = arena.alloc(8 * 512, BF16).rearrange("p (k t) -> p k t", t=512)
        tmp = [arena.alloc(512, F32) for _ in range(4)]
        tt = [arena.alloc(D, F32) for _ in range(2)]
        xr = [arena.alloc(D, F32) for _ in range(2)]
        gpom = cst[:, C_GPOM:C_GPOM + 1024]
        wstb = cstb[:, B_WST:B_WST + 512].rearrange("p (h q) -> p h q", q=128)

        def ld_win(eng, sem):
            for q in range(3):
                eng.dma_start(out=win[:, :, q * 1024:(q + 1) * 1024],
                              in_=w_in[:, 512 + q * 1024:512 + (q + 1) * 1024].rearrange("(k p) c -> p k c", p=128)
                              ).then_inc(sem, 16)
        P.op("pool", ld_win, writes=["win"], ndma=3)
        P.op("pool", dma1(wfo, w_fo.rearrange("(g p) d -> p g d", p=128)), writes=["wfo"], ndma=1)
        P.op("pool", dma1(wso, w_so.rearrange("(g p) d -> p g d", p=128)), writes=["wso"], ndma=1)
        P.op("pool", dma1(wo, w_o.rearrange("(k p) d -> p k d", p=128)), writes=["wo"], ndma=1)
        P.op("pool", lambda e: e.memset(ones, 1.0), writes=["ones"])
        mm_group(bank(1), [(ones, cstb[:, B_WST:B_WST + 512])], reads=["ones", "cstb"], writes=[("ps", 1)])
        for h in range(4):
            P.op("dve", lambda e, h=h: e.scalar_tensor_tensor(
                out=bmat[:, h * 128:(h + 1) * 128], in0=bank(1)[:, h * 128:(h + 1) * 128],
                scalar=cst[:, C_LNB + h:C_LNB + h + 1], in1=cst[:, C_BS + h * 128:C_BS + (h + 1) * 128],
                op0=ALU.mult, op1=ALU.add), reads=[("ps", 1), "cst"], writes=["bmat"])

        def tiles_for(i):
            t0 = i * BLK
            return [(128, xn[t0 + t * 128:t0 + (t + 1) * 128, :], hT[:, :, t * 128:(t + 1) * 128], "hT")
                    for t in range(4)]

        wring = (1, 2)
        ypairs = ((4, 5), (6, 7))
        U0, V0, GA0, GB0 = 0, 512, 1024, 2048
        st = pre_a("b", tiles_for(0), xt, EPS)
        pre_b(st, xs, (0, 3), gpm)
        def b5(i):
            t0 = i * BLK
            for t in range(4):
                yp = ypairs[ring("ypair", 2)]
                k2 = ring("xr", 2)
                rows = slice(t0 + t * 128, t0 + (t + 1) * 128)
                P.op("sp", dma1(xr[k2], xn[rows, :]), writes=[("xr", k2)], ndma=1)
                for dh in range(2):
                    mm_group(bank(yp[dh]), [(mT[:, kc, t * 128:(t + 1) * 128], wo[:, kc, dh * 512:(dh + 1) * 512])
                                            for kc in range(8)],
                             reads=[("mT", kc) for kc in range(8)] + ["wo"], writes=[("ps", yp[dh])])
                s2 = ring("ss2", 4)
                po = bank(yp[0], 2)
                jk = ring("sqjunk", 2)
                P.op("act", lambda e, po=po, s2=s2, jk=jk: e.activation(out=sqjunks[jk], in_=po, func=AF.Square,
                                                                       accum_out=ssq2[s2][:, 0:1]),
                     reads=[("ps", yp[0]), ("ps", yp[1])], writes=[("sqjunk", jk), ("ssq2", s2)])
                P.op("pool", lambda e, s2=s2: e.tensor_scalar(out=rsd2[s2][:, 0:1], in0=ssq2[s2][:, 0:1],
                                                             scalar1=1.0 / D, scalar2=4.0 * EPS, op0=ALU.mult,
                                                             op1=ALU.add),
                     reads=[("ssq2", s2)], writes=[("rsd2", s2)])
                P.op("pool", lambda e, s2=s2: e.tensor_tensor(out=rsd2[s2][:, 0:1], in0=rsd2[s2][:, 0:1],
                                                             in1=neghalf[:, 0:1], op=ALU.pow),
                     reads=[("rsd2", s2), "neghalf"], writes=[("rsd2", s2)])
                k = ring("tt", 2)
                P.op("dve", lambda e, po=po, s2=s2, k=k: e.scalar_tensor_tensor(
                    out=tt[k], in0=po, scalar=rsd2[s2][:, 0:1], in1=gpom, op0=ALU.mult, op1=ALU.mult),
                    reads=[("ps", yp[0]), ("ps", yp[1]), ("rsd2", s2), "cst"], writes=[("tt", k)])
                P.op(ADD_ENG, lambda e, k=k, k2=k2: e.tensor_tensor(out=tt[k], in0=tt[k], in1=xr[k2], op=ALU.add),
                     reads=[("tt", k), ("xr", k2)], writes=[("tt", k)])
                P.op("sp", dma1(x1s[rows, :], tt[k]), reads=[("tt", k)], writes=[("x1s", i, t)], ndma=1)


        for i in range(nblk):
            t0 = i * BLK
            yb = Yb[i % 2]
            P.op("sp", dma1(yb, yscr[:, :, t0:t0 + BLK].rearrange("g p t -> p g t")), reads=[("yscr",)],
                 writes=[("Yb", i % 2)], ndma=1)
            for t in range(4):
                wb = wring[ring("wring", 2)]
                mm_group(bank(wb), [(hT[:, kc, t * 128:(t + 1) * 128], win[:, kc, V0:V0 + 512]) for kc in range(8)],
                         reads=["hT", "win"], writes=[("ps", wb)])
                k = ring("vg", 2)
                P.op("act", lambda e, k=k, wb=wb: e.activation(out=vg[k], in_=bank(wb), func=AF.Gelu_apprx_tanh),
                     reads=[("ps", wb)], writes=[("vg", k)])
                P.op("dve", lambda e, k=k: e.bn_stats(out=st6[k][:, 0:6], in_=vg[k]), reads=[("vg", k)],
                     writes=[("st6", k)])
                P.op("dve", lambda e, k=k: e.bn_aggr(out=mv[k][:, 0:2], in_=st6[k][:, 0:6]), reads=[("st6", k)],
                     writes=[("mv", k)])
                P.op("pool", lambda e, k=k: e.tensor_scalar(out=mv[k][:, 2:3], in0=mv[k][:, 1:2], scalar1=1.0,
                                                           scalar2=EPS, op0=ALU.mult, op1=ALU.add),
                     reads=[("mv", k)], writes=[("mvr", k)])
                P.op("pool", lambda e, k=k: e.tensor_tensor(out=mv[k][:, 2:3], in0=mv[k][:, 2:3],
                                                           in1=neghalf[:, 0:1], op=ALU.pow),
                     reads=[("mvr", k), "neghalf"], writes=[("mvr", k)])
                P.op("dve", lambda e, k=k, t=t: e.tensor_scalar(out=vhat[:, t, :], in0=vg[k], scalar1=mv[k][:, 0:1],
                                                               scalar2=mv[k][:, 2:3], op0=ALU.subtract,
                                                               op1=ALU.mult),
                     reads=[("vg", k), ("mv", k), ("mvr", k)], writes=[("vhat", t)])
            if i > 0:
                b5(i - 1)
            def proj_chunk(off, c, dst, dname, func, scale):
                wb = wring[ring("wring", 2)]
                mm_group(bank(wb), [(win[:, kc, off + c * 128:off + (c + 1) * 128], hT[:, kc, :])
                                    for kc in range(8)],
                         reads=["hT", "win"], writes=[("ps", wb)])
                P.op("act", lambda e: e.activation(out=dst[:, c, :], in_=bank(wb), func=func, scale=scale),
                     reads=[("ps", wb)], writes=[(dname, c)])

            def sgu_group(h):
                wb = wring[ring("wring", 2)]
                for t in range(4):
                    mm_group(bank(wb)[:, t * 128:(t + 1) * 128],
                             [(vhat[:, t, h * 128:(h + 1) * 128], wstb[:, h, :])],
                             reads=[("vhat", t), "cstb"], writes=[("ps", wb)])
                k = ring("tmp", 4)
                P.op("dve", lambda e: e.scalar_tensor_tensor(
                    out=tmp[k].rearrange("p (t q) -> p t q", q=128),
                    in0=bank(wb).rearrange("p (t q) -> p t q", q=128),
                    scalar=cst[:, C_LNG + h:C_LNG + h + 1],
                    in1=bmat[:, h * 128:(h + 1) * 128].unsqueeze(1).broadcast_to([128, 4, 128]),
                    op0=ALU.mult, op1=ALU.add),
                    reads=[("ps", wb), "cst", "bmat"], writes=[("tmp", k)])
                P.op("dve", lambda e: e.tensor_tensor(out=sT[:, h, :], in0=tmp[k], in1=uT[:, h, :], op=ALU.mult),
                     reads=[("tmp", k), ("uT", h)], writes=[("sT", h)])

            for c in range(4):
                proj_chunk(U0, c, uT, "uT", AF.Gelu_apprx_tanh, 1.0)
            nxt = None
            for h in range(4):
                sgu_group(h)
                for c in range(4):
                    cc = (h % 2) * 4 + c
                    if h < 2:
                        proj_chunk(GA0, cc, taT, "taT", AF.Tanh, 0.5)
                    else:
                        proj_chunk(GB0, cc, tbT, "tbT", AF.Tanh, 0.5)
            if i + 1 < nblk:
                nxt = pre_a("b", tiles_for(i + 1), xt, EPS)
            for dc in range(8):
                yp = ypairs[ring("ypair", 2)]
                mm_group(bank(yp[0]), [(wfo[:, g, dc * 128:(dc + 1) * 128], yb[:, g, :]) for g in range(4)],
                         reads=["wfo", ("Yb", i % 2)], writes=[("ps", yp[0])])
                mm_group(bank(yp[1]), [(wso[:, h, dc * 128:(dc + 1) * 128], sT[:, h, :]) for h in range(4)],
                         reads=["wso"] + [("sT", h) for h in range(4)], writes=[("ps", yp[1])])
                k1 = ring("tmp", 4)
                P.op("dve", lambda e, k1=k1, dc=dc, yp=yp: e.scalar_tensor_tensor(
                    out=tmp[k1], in0=taT[:, dc, :], scalar=1.0, in1=bank(yp[0]), op0=ALU.add, op1=ALU.mult),
                    reads=[("taT", dc), ("ps", yp[0])], writes=[("tmp", k1)])
                k2 = ring("tmp", 4)
                P.op("dve", lambda e, k2=k2, dc=dc, yp=yp: e.scalar_tensor_tensor(
                    out=tmp[k2], in0=tbT[:, dc, :], scalar=1.0, in1=bank(yp[1]), op0=ALU.add, op1=ALU.mult),
                    reads=[("tbT", dc), ("ps", yp[1])], writes=[("tmp", k2)])
                P.op(ADD_ENG, lambda e, k1=k1, k2=k2, dc=dc: e.tensor_tensor(out=mT[:, dc, :], in0=tmp[k1],
                                                                          in1=tmp[k2], op=ALU.add),
                     reads=[("tmp", k1), ("tmp", k2)], writes=[("mT", dc)])
            if nxt is not None:
                pre_b(nxt, xs, (0, 3), gpm)
        b5(nblk - 1)

    def phase_c(src):
        arena.release(base_mark)
        nblk = cfg.get("nblk", NBLK)
        NH = 2 * NBLK
        wd = arena.alloc(32 * 1024, BF16).rearrange("p (j d) -> p j d", d=1024)
        NW = 3
        wup = [arena.alloc(2048, BF16).rearrange("p (kc h f) -> p kc h f", kc=8, h=2) for _ in range(NW)]
        xt = [arena.alloc(D, F32) for _ in range(4)]
        xs = [arena.alloc(D, BF16) for _ in range(2)]
        h2T = [arena.alloc(8 * 512, BF16).rearrange("p (k t) -> p k t", t=512) for _ in range(2)]
        actT = arena.alloc(32 * 512, BF16).rearrange("p (j t) -> p j t", t=512)
        cgb = [arena.alloc(512, F32) for _ in range(3)]
        cvb = [arena.alloc(512, F32) for _ in range(3)]
        ggb = [arena.alloc(512, BF16) for _ in range(3)]
        tt = [arena.alloc(D, F32) for _ in range(2)]
        xr = [arena.alloc(D, F32) for _ in range(2)]
        haloU = arena.alloc(64 * NH, F32).rearrange("p (c t) -> p c t", t=NH)
        hhT = arena.alloc(8 * NH, BF16).rearrange("p (k t) -> p k t", t=NH)
        gpof = cst[:, C_GPOF:C_GPOF + 1024]

        P.op("pool", dma1(wd, w_down.rearrange("(j p) d -> p j d", p=128)), writes=["wd"], ndma=1)

        def wload(sl, j):
            P.op("sp", dma1(wup[sl], wups[j].rearrange("p (kc h f) -> p kc h f", kc=8, h=2)),
                 reads=[("wups", j)], writes=[("wup", sl)], ndma=1)

        srcb = src.rearrange("(i r) d -> i r d", r=BLK)
        pieces = [(0, 1, src[0:1, :]),
                  (1, NBLK, srcb[0:NBLK - 1, BLK - 1, :]),
                  (NBLK, 2 * NBLK - 1, srcb[1:NBLK, 0, :]),
                  (2 * NBLK - 1, 2 * NBLK, src[NTOK - 1:NTOK, :])]
        st = pre_a("c", [(NH, pieces, hhT[:, :, :], "hhT")], xt, EPS, hm_col=cst[:, C_HM:C_HM + 1])
        pre_b(st, xs, 0, gpf)
        def tiles_for(i):
            t0 = i * BLK
            hb = h2T[i % 2]
            res = ("h2T", i % 2)
            return [(128, src[t0 + t * 128:t0 + (t + 1) * 128, :], hb[:, :, t * 128:(t + 1) * 128], res)
                    for t in range(4)]

        st0 = pre_a("c", tiles_for(0), xt, EPS)
        pre_b(st0, xs, 0, gpf)
        hbanks = [6, 7]
        PER = 12
        NHW = 8
        actflat = actT.rearrange("p j t -> p (j t)")
        hw = [actflat[:, sl * 2048:(sl + 1) * 2048].rearrange("p (kc h f) -> p kc h f", kc=8, h=2) for sl in range(NHW)]

        def hload(sl, j):
            P.op("sp", dma1(hw[sl], wups[j].rearrange("p (kc h f) -> p kc h f", kc=8, h=2)),
                 reads=[("wups", j)], writes=[("hw", sl)], ndma=1)
        for j in range(NHW):
            hload(j % NHW, j)
        for c0 in range(0, 64, PER):
            hbk = hbanks[(c0 // PER) % 2]
            cs_ = list(range(c0, min(c0 + PER, 64)))
            for c in cs_:
                j, half = c // 2, c % 2
                sl = j % NHW
                mm_group(bank(hbk)[:, (c - c0) * NH:(c - c0 + 1) * NH],
                         [(hw[sl][:, kc, half, :], hhT[:, kc, :]) for kc in range(8)],
                         reads=[("hw", sl), "hhT"], writes=[("ps", hbk)])
                if half == 1 and j + NHW < 32:
                    hload((j + NHW) % NHW, j + NHW)
            n = len(cs_)
            for half in range(2):
                idx = [ci for ci in cs_ if ci % 2 == half]
                j0 = idx[0] // 2
                srcv = bank(hbk)[:, 0:n * NH].rearrange("p (c t) -> p c t", t=NH)
                first = idx[0] - c0
                evac("dve" if half == 0 else "act",
                     haloU[:, half * 32 + j0:half * 32 + j0 + len(idx), :],
                     srcv[:, first:n:2, :], reads=[("ps", hbk)], writes=["haloU"])

        cw3 = cst[:, C_CW:C_CW + 192].rearrange("p (c t) -> p c t", t=3)
        for side in range(2):
            hv = haloU[:, :, side * NBLK:(side + 1) * NBLK]
            wsd = cw3[:, :, 2 * side:2 * side + 1].broadcast_to([128, 64, NBLK])
            cbb = cst[:, C_CB:C_CB + 64].unsqueeze(2).broadcast_to([128, 64, NBLK])
            P.op("dve", lambda e, hv=hv, wsd=wsd: e.tensor_tensor(out=hv, in0=hv, in1=wsd, op=ALU.mult),
                 reads=["haloU", "cst"], writes=["haloU"])
            P.op("dve", lambda e, hv=hv, cbb=cbb: e.tensor_tensor(out=hv, in0=hv, in1=cbb, op=ALU.add),
                 reads=["haloU", "cst"], writes=["haloU"])

        chunks = [(i, j) for i in range(nblk) for j in range(32)]
        loaded = [0]
        wslot = {}

        def ensure_loaded(upto):
            while loaded[0] <= min(upto, len(chunks) - 1):
                n = loaded[0]
                i, j = chunks[n]
                wslot[n] = n % NW
                wload(n % NW, j)
                loaded[0] += 1

        uppairs = [(1, 2), (3, 4), (5, 6)]
        pdslots = [5, 1, 3]

        pending_fin = [None]
        for i in range(nblk):
            hb = h2T[i % 2]
            hres = ("h2T", i % 2)
            nxt = None
            for j in range(32):
                n = i * 32 + j
                ensure_loaded(n + NW - 1)
                sl = wslot[n]
                w = wup[sl]
                pr = uppairs[ring("uppair", 3)]
                for half in range(2):
                    b = pr[half]
                    mm_group(bank(b), [(w[:, kc, half, :], hb[:, kc, :]) for kc in range(8)],
                             reads=[("wup", sl), hres], writes=[("ps", b)])
                k = ring("cg", 3)
                cg, cv, gg = cgb[k], cvb[k], ggb[k]
                trip = ((0, cg, "cg"), (1, cv, "cv"))
                for half, cbuf, cn in trip:
                    b = pr[half]
                    c = half * 32 + j
                    w1 = cst[:, C_CW + 3 * c + 1:C_CW + 3 * c + 2]
                    bb = cst[:, C_CB + c:C_CB + c + 1]
                    P.op("act", lambda e, cbuf=cbuf, b=b, w1=w1, bb=bb: e.activation(
                        out=cbuf[:, 1:511], in_=bank(b)[:, 1:511], func=AF.Identity, bias=bb, scale=w1),
                        reads=[("ps", b), "cst"], writes=[(cn, k, "m")])
                    for side in range(2):
                        col = 511 * side
                        hv = haloU[:, c, side * NBLK + i:side * NBLK + i + 1]
                        P.op("act", lambda e, cbuf=cbuf, b=b, w1=w1, hv=hv, col=col: e.activation(
                            out=cbuf[:, col:col + 1], in_=bank(b)[:, col:col + 1], func=AF.Identity, bias=hv,
                            scale=w1),
                            reads=[("ps", b), "cst", "haloU"], writes=[(cn, k, side)])
                for side in range(2):
                    for half, cbuf, cn in trip:
                        b = pr[half]
                        c = half * 32 + j
                        wt = cst[:, C_CW + 3 * c + 2 * side:C_CW + 3 * c + 2 * side + 1]
                        pb = bank(b)
                        o_ = cbuf[:, 1:512] if side == 0 else cbuf[:, 0:511]
                        i_ = pb[:, 0:511] if side == 0 else pb[:, 1:512]
                        rr = [(cn, k, "m"), (cn, k, 1 - side)]
                        P.op("dve", lambda e, o_=o_, i_=i_, wt=wt: e.scalar_tensor_tensor(
                            out=o_, in0=i_, scalar=wt, in1=o_, op0=ALU.mult, op1=ALU.add),
                            reads=[("ps", b), "cst"] + rr, writes=rr)
                def finish(k=k, cg=cg, cv=cv, gg=gg, j=j):
                    P.op("act", lambda e: e.activation(out=gg, in_=cg, func=AF.Gelu_apprx_tanh),
                         reads=[("cg", k, "m"), ("cg", k, 0), ("cg", k, 1)], writes=[("gg", k)])
                    P.op(MUL_ENG, lambda e: e.tensor_tensor(out=actT[:, j, :], in0=gg, in1=cv, op=ALU.mult),
                         reads=[("gg", k), ("cv", k, "m"), ("cv", k, 0), ("cv", k, 1)], writes=[("act", j)])
                if pending_fin[0] is not None:
                    pending_fin[0]()
                pending_fin[0] = finish
            pending_fin[0]()
            pending_fin[0] = None
            t0 = i * BLK
            if i + 1 < nblk:
                nxt = pre_a("c", tiles_for(i + 1), xt, EPS)
            r = ring_ctr.get("uppair", 0) % 3
            tbanks = [uppairs[r][0], uppairs[(r + 1) % 3][0], uppairs[(r + 2) % 3][0], uppairs[(r + 1) % 3][0]]

            def dpart(t, j0, j1):
                b0 = tbanks[t]
                for dh in range(2):
                    P.op("pe", lambda e, t=t, dh=dh, b0=b0, j0=j0, j1=j1: [e.matmul(
                        bank(b0 + dh), lhsT=actT[:, j, t * 128:(t + 1) * 128],
                        rhs=wd[:, j, dh * 512:(dh + 1) * 512], start=(j == 0), stop=(j == 31))
                        for j in range(j0, j1)][-1],
                        reads=[("act", j) for j in range(j0, j1)] + ["wd"], writes=[("ps", b0 + dh)])

            def depi(t):
                b0 = tbanks[t]
                k2 = ring("xr", 2)
                rows = slice(t0 + t * 128, t0 + (t + 1) * 128)
                P.op("sp", dma1(xr[k2], src[rows, :]), writes=[("xr", k2)], ndma=1)
                s2 = ring("ss2", 4)
                pd = bank(b0, 2)
                jk = ring("sqjunk", 2)
                P.op("act", lambda e: e.activation(out=sqjunks[jk], in_=pd, func=AF.Square,
                                                   accum_out=ssq2[s2][:, 0:1]),
                     reads=[("ps", b0), ("ps", b0 + 1)], writes=[("sqjunk", jk), ("ssq2", s2)])
                P.op("pool", lambda e: e.tensor_scalar(out=rsd2[s2][:, 0:1], in0=ssq2[s2][:, 0:1],
                                                       scalar1=1.0 / D, scalar2=EPS, op0=ALU.mult, op1=ALU.add),
                     reads=[("ssq2", s2)], writes=[("rsd2", s2)])
                P.op("pool", lambda e: e.tensor_tensor(out=rsd2[s2][:, 0:1], in0=rsd2[s2][:, 0:1],
                                                       in1=neghalf[:, 0:1], op=ALU.pow),
                     reads=[("rsd2", s2), "neghalf"], writes=[("rsd2", s2)])
                k = ring("tt", 2)
                P.op("dve", lambda e: e.scalar_tensor_tensor(
                    out=tt[k], in0=pd, scalar=rsd2[s2][:, 0:1], in1=gpof, op0=ALU.mult, op1=ALU.mult),
                    reads=[("ps", b0), ("ps", b0 + 1), ("rsd2", s2), "cst"], writes=[("tt", k)])
                P.op(ADD_ENG, lambda e: e.tensor_tensor(out=tt[k], in0=tt[k], in1=xr[k2], op=ALU.add),
                     reads=[("tt", k), ("xr", k2)], writes=[("tt", k)])
                P.op("sp", dma1(y[rows, :], tt[k]), reads=[("tt", k)], writes=[("y", i, t)], ndma=1)

            JA = 22
            for t in range(3):
                dpart(t, 0, JA)
                if t == 1 and nxt is not None:
                    pre_b(nxt, xs, 0, gpf)
            cuts = [JA, 26, 28, 30, 31, 32]
            for t in range(3):
                for ci in range(len(cuts) - 1):
                    dpart(t, cuts[ci], cuts[ci + 1])
                depi(t)
            dpart(3, 0, 32)
            depi(3)

    if "A" in phases:
        phase_a()
        P.barrier()
    if "B" in phases:
        phase_b()
        P.barrier()
    if "C" in phases:
        if "A" not in phases:
            cast_wup()
        phase_c(xn if cfg.get("c_src") == "xn" else x1s)
    P.emit()
    return nc


def _core_kind(c):
    return "sample" if c < 2 else "prompt"


def _core_tokens(c, x_prompt, x_sample):
    if c < 2:
        return np.concatenate([x_sample[c], x_prompt[c]], axis=0)
    b0 = 2 + 5 * (c - 2)
    return x_prompt[b0:b0 + 5].reshape(NTOK, D)


_TABLES = {}


def _tables(kind):
    if kind not in _TABLES:
        _TABLES[kind] = _dft_tables(kind)
    return _TABLES[kind]


def make_in_maps(inputs, cores):
    f = lambda a: np.ascontiguousarray(np.asarray(a, dtype=np.float32))
    x_prompt = f(inputs["x_prompt"])
    x_sample = f(inputs["x_sample"])
    w_in = f(inputs["w_in"][0])
    shared = {
        "w_in": w_in,
        "w_fo": f(inputs["w_fourier_out"][0]),
        "w_so": f(inputs["w_sgu_out"][0]),
        "w_o": f(inputs["w_o"][0]),
        "w_up": f(inputs["w_up"][0]),
        "w_down": f(inputs["w_down"][0]),
    }
    cst0 = np.zeros((128, NCST), dtype=np.float32)
    col = lambda v: f(v).reshape(8, 128).T
    cst0[:, C_GPM:C_GPM + 8] = col(inputs["norm_pre_mix"][0])
    cst0[:, C_GPF:C_GPF + 8] = col(inputs["norm_pre_ffn"][0])
    cst0[:, C_GPOM:C_GPOM + 1024] = f(inputs["norm_post_mix"][0])[None, :]
    cst0[:, C_GPOF:C_GPOF + 1024] = f(inputs["norm_post_ffn"][0])[None, :]
    cw = f(inputs["conv_w"][0])
    cst0[:, C_CW:C_CW + 192] = cw.reshape(3, 64, 128).transpose(2, 1, 0).reshape(128, 192)
    cst0[:, C_CB:C_CB + 64] = f(inputs["conv_b"][0]).reshape(64, 128).T
    cst0[:, C_LNG:C_LNG + 4] = f(inputs["sgu_ln_g"][0]).reshape(4, 128).T
    cst0[:, C_LNB:C_LNB + 4] = f(inputs["sgu_ln_b"][0]).reshape(4, 128).T
    cst0[:, C_BS:C_BS + 512] = f(inputs["sgu_b_s"][0]).reshape(1, 512)
    wst = f(inputs["sgu_w_s"][0]).transpose(2, 0, 1).reshape(128, 512)
    cst0[:, C_WST:C_WST + 512] = wst
    maps = []
    for c in cores:
        kind = _core_kind(c)
        t8, t2, cs, m8, m2 = _tables(kind)
        xn = np.ascontiguousarray(_core_tokens(c, x_prompt, x_sample))
        xfo = np.ascontiguousarray(xn[_fourier_row_order(kind)])
        cst = cst0.copy()
        cst[:, C_HM:C_HM + 1] = _halo_mask(kind)
        cstb = np.zeros((128, NCSTB), dtype=np.float32)
        cstb[:, B_ID:B_ID + 128] = np.eye(128, dtype=np.float32)
        cstb[:, B_CS:B_CS + 512] = cs
        cstb[:, B_M8:B_M8 + 256] = m8
        cstb[:, B_M2:B_M2 + 256] = m2
        cstb[:, B_WST:B_WST + 512] = wst
        m = dict(shared)
        m.update({"xn": xn, "xf": xfo, "cst": cst, "cstb": cstb, "t8": t8, "t2": t2})
        maps.append(m)
    return maps


def assemble_outputs(ys):
    y_prompt = np.empty((32, 2048, D), dtype=np.float32)
    y_sample = np.empty((2, 8192, D), dtype=np.float32)
    for c in range(NCORES):
        yc = ys[c]
        if c < 2:
            y_sample[c] = yc[:S8]
            y_prompt[c] = yc[S8:]
        else:
            b0 = 2 + 5 * (c - 2)
            y_prompt[b0:b0 + 5] = yc.reshape(5, 2048, D)
    return y_prompt, y_sample


def kernel(**inputs):
    nc = build_program()
    maps = make_in_maps(inputs, list(range(NCORES)))
    res = run_bass_kernel_spmd(nc, maps, core_ids=list(range(NCORES)))
    ys = [np.asarray(r["y"], dtype=np.float32) for r in res.results]
    return assemble_outputs(ys)
```

```python
import numpy as np
import concourse.bass as bass
import concourse.mybir as mybir
from concourse.bass_utils import run_bass_kernel_spmd

_BF16NP = mybir.dt.np(mybir.dt.bfloat16)

F32 = mybir.dt.float32
BF16 = mybir.dt.bfloat16
AF = mybir.ActivationFunctionType
ALU = mybir.AluOpType

D = 1024
NTOK = 10240
S8 = 8192
S2 = 2048
BLK = 512
NBLK = NTOK // BLK
DFF = 4096
EPS = 1e-6
NCORES = 8

ENGS = ("pe", "act", "dve", "pool", "sp")
NDMA_SEMS = 44
NDMA_HW = 32


class Op:
    __slots__ = ("eng", "fn", "deps", "signal", "count", "is_dma", "ndma", "idx", "eidx", "sem", "semval")


class Prog:
    def __init__(self, nc):
        self.nc = nc
        self.ops = []
        self.eng_ops = {e: [] for e in ENGS}
        self.last_writer = {}
        self.readers = {}
        self.seen = {e: {} for e in ENGS}
        self.seen_dma = {e: set() for e in ENGS}
        self.dma_last = [None] * NDMA_SEMS
        self.dma_use = [0] * NDMA_SEMS
        self.dma_rr = 0
        self.dma_rr_sw = 0

    def op(self, eng, fn, reads=(), writes=(), ndma=0):
        o = Op()
        o.eng = eng
        o.fn = fn
        o.signal = False
        o.count = 0
        o.is_dma = ndma > 0
        o.ndma = ndma
        o.idx = len(self.ops)
        o.eidx = len(self.eng_ops[eng])
        o.sem = None
        o.semval = 0
        deps = {}
        for r in reads:
            w = self.last_writer.get(r)
            if w is not None and not (w.eng == "pe" and eng == "pe" and not w.is_dma):
                deps[w.idx] = w
            self.readers.setdefault(r, []).append(o)
        for r in writes:
            w = self.last_writer.get(r)
            if w is not None and w is not o and not (w.eng == "pe" and eng == "pe" and not w.is_dma):
                deps[w.idx] = w
            for rd in self.readers.get(r, ()):
                if rd is not o:
                    deps[rd.idx] = rd
            self.last_writer[r] = o
            self.readers[r] = []
        if o.is_dma:
            if eng == "pool":
                s = NDMA_HW + self.dma_rr_sw
                self.dma_rr_sw = (self.dma_rr_sw + 1) % (NDMA_SEMS - NDMA_HW)
            else:
                s = self.dma_rr
                self.dma_rr = (self.dma_rr + 1) % NDMA_HW
            prev = self.dma_last[s]
            if prev is not None:
                deps[prev.idx] = prev
            self.dma_use[s] += 16 * ndma
            o.sem = s
            o.semval = self.dma_use[s]
            self.dma_last[s] = o
        keep = []
        best = {}
        for d in deps.values():
            if d.is_dma:
                if d.idx in self.seen_dma[eng]:
                    continue
                self.seen_dma[eng].add(d.idx)
                keep.append(d)
            else:
                if self.seen[eng].get(d.eng, -1) >= d.eidx:
                    continue
                if d.eng not in best or best[d.eng].eidx < d.eidx:
                    best[d.eng] = d
        for e, d in best.items():
            self.seen[eng][e] = d.eidx
            keep.append(d)
        for d in keep:
            d.signal = True
        o.deps = keep
        self.ops.append(o)
        self.eng_ops[eng].append(o)
        return o

    def barrier(self):
        dmas = [d for d in self.dma_last if d is not None]
        for e in ENGS:
            o = Op()
            o.eng = e
            o.fn = None
            o.signal = False
            o.count = 0
            o.is_dma = False
            o.ndma = 0
            o.idx = len(self.ops)
            o.eidx = len(self.eng_ops[e])
            o.sem = None
            o.semval = 0
            keep = []
            for e2 in ENGS:
                lst = self.eng_ops[e2]
                k = len(lst) - 1
                while k >= 0 and (lst[k].is_dma or lst[k].fn is None):
                    k -= 1
                if k >= 0 and e2 != e:
                    d = lst[k]
                    if self.seen[e].get(e2, -1) < d.eidx:
                        self.seen[e][e2] = d.eidx
                        keep.append(d)
            for d in dmas:
                if d.idx not in self.seen_dma[e]:
                    self.seen_dma[e].add(d.idx)
                    keep.append(d)
            for d in keep:
                d.signal = True
            o.deps = keep
            self.ops.append(o)
            self.eng_ops[e].append(o)
        self.last_writer = {}
        self.readers = {}

    def emit(self, final_wait_engine="sp"):
        nc = self.nc
        tail = [d for d in self.dma_last if d is not None]
        import contextlib
        with contextlib.ExitStack() as es:
            eng_sem = {e: es.enter_context(nc.semaphore("sem_" + e)) for e in ENGS}
            dma_sems = [es.enter_context(nc.semaphore("dsem%d" % i)) for i in range(NDMA_SEMS)]
            for e in ENGS:
                c = 0
                for o in self.eng_ops[e]:
                    if o.signal and not o.is_dma:
                        c += 1
                        o.count = c
            block = es.enter_context(nc.Block())

            def run(ename, eng):
                for o in self.eng_ops[ename]:
                    for d in o.deps:
                        if d.is_dma:
                            eng.wait_ge(dma_sems[d.sem], d.semval)
                        else:
                            eng.wait_ge(eng_sem[d.eng], d.count)
                    if o.fn is None:
                        assert not o.signal
                        continue
                    if o.is_dma:
                        o.fn(eng, dma_sems[o.sem])
                    else:
                        ins = o.fn(eng)
                        if o.signal:
                            ins.then_inc(eng_sem[ename], 1)
                if ename == final_wait_engine:
                    for d in tail:
                        eng.wait_ge(dma_sems[d.sem], d.semval)

            block.tensor(lambda eng: run("pe", eng))
            block.scalar(lambda eng: run("act", eng))
            block.vector(lambda eng: run("dve", eng))
            block.gpsimd(lambda eng: run("pool", eng))
            block.sync(lambda eng: run("sp", eng))


class Arena:
    def __init__(self, nc, nbytes):
        self.t = nc.alloc_sbuf_tensor("arena", [128, nbytes // 4], F32)
        self.nbytes = nbytes
        self.off = 0

    def mark(self):
        return self.off

    def release(self, m):
        self.off = m

    def alloc(self, n, dtype):
        nb = n * (2 if dtype == BF16 else 4)
        nb = (nb + 31) // 32 * 32
        assert self.off + nb <= self.nbytes, ("arena overflow", self.off, nb, self.nbytes)
        ap = self.t[:, self.off // 4:(self.off + nb) // 4]
        self.off += nb
        if dtype == BF16:
            ap = ap.bitcast(BF16)
        return ap[:, 0:n]


C_GPM, C_GPF = 0, 8
C_GPOM = 16
C_GPOF = C_GPOM + 1024
C_CW = C_GPOF + 1024
C_CB = C_CW + 192
C_LNG = C_CB + 64
C_LNB = C_LNG + 4
C_BS = C_LNB + 4
C_WST = C_BS + 512
C_HM = C_WST + 512
NCST = C_HM + 1
NCST = (NCST + 7) // 8 * 8
B_ID = 0
B_CS = 128
B_M8 = B_CS + 512
B_M2 = B_M8 + 256
B_WST = B_M2 + 256
NCSTB = B_WST + 512


def _dft_tables(kind):
    rho = np.arange(128)[:, None, None].astype(np.float64)
    kap = np.arange(128)[None, None, :].astype(np.float64)
    tau = np.arange(64)[None, :, None].astype(np.float64)
    if kind == "sample":
        psi = 2 * np.pi * (rho * kap / 128.0 + tau * kap / 8192.0)
    else:
        psi = 2 * np.pi * (rho * kap / 128.0 + (tau % 16) * kap / 2048.0)
    sc = 1.0 / np.sqrt(128.0)
    t8 = np.concatenate([np.cos(psi) * sc, -np.sin(psi) * sc], axis=2)
    t = np.arange(64)
    if kind == "sample":
        th = 2 * np.pi * np.outer(t, t) / 64.0
        mc = np.cos(th) / 8.0
        ms = np.sin(th) / 8.0
    else:
        same = (t[:, None] // 16) == (t[None, :] // 16)
        th = 2 * np.pi * np.outer(t % 16, t % 16) / 16.0
        mc = np.where(same, np.cos(th), 0.0) / 4.0
        ms = np.where(same, np.sin(th), 0.0) / 4.0
    eye2 = np.eye(2)
    m8c = np.kron(mc, eye2)
    m8s = np.kron(ms, eye2)
    tau2 = np.arange(16)[None, :, None].astype(np.float64)
    psi2 = 2 * np.pi * (rho * kap / 128.0 + tau2 * kap / 2048.0)
    t2 = np.concatenate([np.cos(psi2) * sc, -np.sin(psi2) * sc], axis=2)
    t16 = np.arange(16)
    th2 = 2 * np.pi * np.outer(t16, t16) / 16.0
    eye8 = np.eye(8)
    m2c = np.kron(np.cos(th2) / 4.0, eye8)
    m2s = np.kron(np.sin(th2) / 4.0, eye8)
    c = np.arange(128)
    ph = 2 * np.pi * np.outer(c, c) / 128.0
    C = np.cos(ph) * sc
    S = np.sin(ph) * sc
    cs = np.concatenate([C, -S, S, C], axis=1)
    return (t8.reshape(128, 64 * 256).astype(_BF16NP), t2.reshape(128, 16 * 256).astype(_BF16NP),
            cs.astype(np.float32), np.concatenate([m8c, m8s], axis=1).astype(np.float32),
            np.concatenate([m2c, m2s], axis=1).astype(np.float32))


def _fourier_row_order(kind):
    idx = np.empty(NTOK, dtype=np.int64)
    tau = np.arange(64)[:, None]
    rho = np.arange(128)[None, :]
    if kind == "sample":
        tok = tau + 64 * rho
    else:
        tok = 2048 * (tau // 16) + (tau % 16) + 16 * rho
    idx[:S8] = tok.reshape(-1)
    tau2 = np.arange(16)[:, None]
    idx[S8:] = (S8 + tau2 + 16 * rho).reshape(-1)
    return idx


def _halo_mask(kind):
    hm = np.zeros((128, 1), dtype=np.float32)
    if kind == "sample":
        bounds = [0, S8, NTOK]
    else:
        bounds = list(range(0, NTOK + 1, 2048))
    for i in range(NBLK):
        t0 = i * BLK
        hm[i, 0] = 0.0 if t0 in bounds else 1.0
        hm[NBLK + i, 0] = 0.0 if (t0 + BLK) in bounds else 1.0
    return hm


def build_program(cfg=None):
    cfg = cfg or {}
    MUL_ENG = cfg.get("mul_eng", "pool")
    ADD_ENG = cfg.get("add_eng", "pool")
    ST_ENG = cfg.get("st_eng", "pool")
    phases = cfg.get("phases", "ABC")
    dbg = cfg.get("debug", False)
    nc = bass.Bass("TRN2", target_bir_lowering=False)

    def din(name, shape):
        return nc.dram_tensor(name, shape, F32, kind="ExternalInput").ap()

    xn = din("xn", [NTOK, D])
    xf = din("xf", [NTOK, D])
    w_in = din("w_in", [D, 3584])
    w_fo = din("w_fo", [512, D])
    w_so = din("w_so", [512, D])
    w_o = din("w_o", [D, D])
    w_up = din("w_up", [D, 2 * DFF])
    w_down = din("w_down", [DFF, D])
    cst_d = din("cst", [128, NCST])
    cstb_d = din("cstb", [128, NCSTB])
    t8_d = nc.dram_tensor("t8", [128, 64 * 256], BF16, kind="ExternalInput").ap()
    t2_d = nc.dram_tensor("t2", [128, 16 * 256], BF16, kind="ExternalInput").ap()
    y = nc.dram_tensor("y", [NTOK, D], F32, kind="ExternalOutput").ap()
    kscr = "ExternalOutput" if dbg else "Internal"
    yscr = nc.dram_tensor("yscr", [4, 128, NTOK], BF16, kind=kscr).ap()
    x1s = nc.dram_tensor("x1s", [NTOK, D], F32, kind=kscr).ap()
    wups = nc.dram_tensor("wups", [32, 128, 2048], BF16, kind="Internal").ap()

    P = Prog(nc)
    arena = Arena(nc, 212000)
    psum = nc.alloc_psum_tensor("psum", [128, 4096], F32)

    def bank(b, n=1):
        return psum[:, 512 * b:512 * (b + n)]

    def bankb(b):
        return psum[:, 512 * b:512 * (b + 1)].bitcast(BF16)

    cst = arena.alloc(NCST, F32)
    cstb = arena.alloc(NCSTB, BF16)
    neghalf = arena.alloc(8, F32)
    ssq = [arena.alloc(8, F32) for _ in range(4)]
    rsd = [arena.alloc(8, F32) for _ in range(4)]
    ssq2 = [arena.alloc(8, F32) for _ in range(4)]
    rsd2 = [arena.alloc(8, F32) for _ in range(4)]
    ident = cstb[:, B_ID:B_ID + 128]
    gpm = cst[:, C_GPM:C_GPM + 8]
    gpf = cst[:, C_GPF:C_GPF + 8]

    def dma1(out, in_, **kw):
        def f(eng, sem):
            eng.dma_start(out=out, in_=in_, **kw).then_inc(sem, 16)
        return f

    P.op("sp", dma1(cst, cst_d), writes=["cst"], ndma=1)
    P.op("pool", dma1(cstb, cstb_d), writes=["cstb"], ndma=1)
    P.op("pool", lambda e: e.memset(neghalf, -0.5), writes=["neghalf"])
    for i in range(4):
        P.op("pool", lambda e, i=i: e.memset(ssq[i], 1.0), writes=[("ssq", i, t) for t in range(8)])
    sqjunks = [arena.alloc(D, BF16) for _ in range(2)]

    def cast_wup():
        wv = w_up.rearrange("(kc p) (h j f) -> j p kc h f", p=128, h=2, f=128)
        for j in range(32):
            ov = wups[j].rearrange("p (kc h f) -> p kc h f", kc=8, h=2)

            def f(eng, sem, j=j, ov=ov):
                for h in range(2):
                    eng.dma_start(out=ov[:, :, h, :], in_=wv[j][:, :, h, :]).then_inc(sem, 16)
            P.op("pool", f, writes=[("wups", j)], ndma=2)

    ring_ctr = {}

    def ring(name, n):
        c = ring_ctr.get(name, 0)
        ring_ctr[name] = c + 1
        return c % n

    def pre_a(tag, tiles, xt_bufs, eps, hm_col=None):
        assert len(xt_bufs) >= len(tiles)
        s = ring(tag + "ss", 4)
        ss, rs = ssq[s], rsd[s]
        st = []
        for ti, (Pn, src, dst, dres) in enumerate(tiles):
            k = ring(tag + "xt", len(xt_bufs))
            xt = xt_bufs[k]
            st.append((Pn, src, dst, dres, k))
            pieces = src if isinstance(src, list) else [(0, Pn, src)]

            def ld(eng, sem, xt=xt, pieces=pieces):
                for (p0, p1, ap) in pieces:
                    eng.dma_start(out=xt[p0:p1, :], in_=ap).then_inc(sem, 16)
            P.op("sp", ld, writes=[(tag + "xt", k)], ndma=len(pieces))
        for ti, (Pn, src, dst, dres, k) in enumerate(st):
            xt = xt_bufs[k]
            jk = ring("sqjunk", 2)
            P.op("act", lambda e, xt=xt, Pn=Pn, ti=ti, ss=ss, jk=jk: e.activation(
                out=sqjunks[jk][:Pn, :], in_=xt[:Pn, :], func=AF.Square, accum_out=ss[:Pn, ti:ti + 1]),
                reads=[(tag + "xt", k)], writes=[("sqjunk", jk), ("ssq", s, ti)])
        nt = len(tiles)
        P.op("pool", lambda e: e.tensor_scalar(out=rs[:, 0:nt], in0=ss[:, 0:nt], scalar1=1.0 / D, scalar2=eps,
                                               op0=ALU.mult, op1=ALU.add),
             reads=[("ssq", s, ti) for ti in range(nt)], writes=[("rsd", s)])
        P.op("pool", lambda e: e.tensor_tensor(out=rs[:, 0:nt], in0=rs[:, 0:nt], in1=neghalf[:, 0:nt], op=ALU.pow),
             reads=[("rsd", s), "neghalf"], writes=[("rsd", s)])
        if hm_col is not None:
            P.op("pool", lambda e: e.tensor_tensor(out=rs[:, nt - 1:nt], in0=rs[:, nt - 1:nt], in1=hm_col,
                                                   op=ALU.mult),
                 reads=[("rsd", s), "cst"], writes=[("rsd", s)])
        return (tag, st, s, xt_bufs)

    def pre_b(state, xs_bufs, tbank, gvec):
        tag, st, s, xt_bufs = state
        rs = rsd[s]
        for ti, (Pn, src, dst, dres, k) in enumerate(st):
            xt = xt_bufs[k]
            k2 = ring(tag + "xs", len(xs_bufs))
            xs = xs_bufs[k2]
            P.op("act", lambda e, xt=xt, xs=xs, Pn=Pn, ti=ti: e.activation(
                out=xs[:Pn, :], in_=xt[:Pn, :], func=AF.Copy, scale=rs[:Pn, ti:ti + 1]),
                reads=[(tag + "xt", k), ("rsd", s)], writes=[(tag + "xs", k2)])
            tbk = tbank[ti % len(tbank)] if isinstance(tbank, tuple) else tbank
            tb = bankb(tbk)

            def tr(e, xs=xs, Pn=Pn, tb=tb):
                ins = None
                for kc in range(8):
                    ins = e.transpose(out=tb[:, kc * 128:kc * 128 + Pn], in_=xs[:Pn, kc * 128:(kc + 1) * 128],
                                      identity=ident[:Pn, :Pn])
                return ins
            P.op("pe", tr, reads=[(tag + "xs", k2), "cstb"], writes=[("ps", tbk)])
            tb3 = tb.rearrange("p (k t) -> p k t", t=128)
            P.op("dve", lambda e, dst=dst, tb3=tb3, Pn=Pn: e.tensor_tensor(
                out=dst, in0=tb3[:, :, 0:Pn], in1=gvec.unsqueeze(2).broadcast_to([128, 8, Pn]), op=ALU.mult),
                reads=[("ps", tbk), "cst"], writes=[dres])

    def mm_group(out, pairs, reads, writes):
        def f(e):
            ins = None
            n = len(pairs)
            for i, (l, r) in enumerate(pairs):
                ins = e.matmul(out, lhsT=l, rhs=r, start=(i == 0), stop=(i == n - 1))
            return ins
        return P.op("pe", f, reads=reads, writes=writes)

    def evac(eng, out, in_, reads, writes):
        if eng == "act":
            return P.op("act", lambda e: e.activation(out=out, in_=in_, func=AF.Copy), reads=reads, writes=writes)
        return P.op("dve", lambda e: e.tensor_copy(out=out, in_=in_), reads=reads, writes=writes)

    base_mark = arena.mark()

    def both(name):
        return [(name, "act"), (name, "dve")]

    alt = [0]

    def alt_eng():
        alt[0] ^= 1
        return "act" if alt[0] else "dve"

    def phase_a():
        arena.release(base_mark)
        tcs = arena.alloc(64 * 256, BF16)
        Fb = arena.alloc(64 * 512, BF16).rearrange("p (t c) -> p t c", c=512)
        Gb0 = arena.alloc(2 * 64 * 128, BF16)
        YT = arena.alloc(S8, BF16)
        Bb = [arena.alloc(256, BF16) for _ in range(4)]
        ovl_mark = arena.mark()
        wf = arena.alloc(8 * 512, BF16).rearrange("p (k c) -> p k c", c=512)
        xt = [arena.alloc(D, F32) for _ in range(4)]
        xs = [arena.alloc(D, BF16) for _ in range(2)]
        hfT = [arena.alloc(8 * 256, BF16).rearrange("p (k t) -> p k t", t=256) for _ in range(2)]
        end_mark = arena.mark()
        arena.release(ovl_mark)
        Gb1 = arena.alloc(2 * 64 * 128, BF16)
        assert arena.mark() <= end_mark
        arena.release(end_mark)
        ovl_res = ["wf"] + [("axt", k) for k in range(4)] + [("axs", k) for k in range(2)] + \
                  [("hfT", k) for k in range(2)]
        cs1 = cstb[:, B_CS:B_CS + 256]
        cs2 = cstb[:, B_CS + 256:B_CS + 512]

        P.op("pool", dma1(wf, w_in[:, 0:512].rearrange("(k p) c -> p k c", p=128)), writes=["wf"], ndma=1)

        for (tok0, NT1, t_d, mcol, Gbufs) in ((S8, 16, t2_d, B_M2, [Gb0]), (0, 64, t8_d, B_M8, [Gb0, Gb1])):
            E = 128 // NT1
            NB = NT1
            ncol = NT1 * 256
            P.op("sp", dma1(tcs[:, 0:ncol], t_d), writes=["tcs"], ndma=1)
            tc3 = tcs[:, 0:ncol].rearrange("p (t k) -> p t k", k=256)
            mc = cstb[:, mcol:mcol + 128]
            ms = cstb[:, mcol + 128:mcol + 256]
            G5s = [G[:, 0:2 * NB * 128].rearrange("p (a b t e) -> p a b t e", a=2, b=NB, t=NT1, e=E) for G in Gbufs]
            G3s = [G[:, 0:2 * NB * 128].rearrange("p (a b m) -> p a b m", a=2, b=NB) for G in Gbufs]
            YT4 = YT[:, 0:NT1 * 128].rearrange("p (k b e) -> p k b e", b=NB, e=E)
            ng = len(Gbufs)

            def tiles_for(pi):
                hb = hfT[pi % 2]
                res = ("hfT", pi % 2)
                return [(128, xf[tok0 + (2 * pi + u) * 128:tok0 + (2 * pi + u + 1) * 128, :],
                         hb[:, :, u * 128:(u + 1) * 128], res) for u in range(2)]
            npair = NT1 // 2
            sts = {0: pre_a("a", tiles_for(0), xt, EPS)}
            pre_b(sts[0], xs, (0, 7), gpm)
            if npair > 1:
                sts[1] = pre_a("a", tiles_for(1), xt, EPS)
            for pi in range(npair):
                if pi + 1 < npair:
                    pre_b(sts[pi + 1], xs, (0, 7), gpm)
                if pi + 2 < npair:
                    sts[pi + 2] = pre_a("a", tiles_for(pi + 2), xt, EPS)
                hb = hfT[pi % 2]
                for u in range(2):
                    tau = 2 * pi + u
                    fb = (1, 2)[ring("fbank", 2)]
                    mm_group(bank(fb), [(hb[:, kc, u * 128:(u + 1) * 128], wf[:, kc, :]) for kc in range(8)],
                             reads=[("hfT", pi % 2), "wf"], writes=[("ps", fb)])
                    evac("dve", Fb[:, tau, :], bank(fb), reads=[("ps", fb)], writes=[("F", "dve")])

            if tok0 == 0 and "C" in phases:
                cast_wup()

            def stage1(g, tau):
                gi = g % ng
                sb = (3, 4, 2)[ring("s1bank", 3)]
                mm_group(bank(sb)[:, 0:256], [(Fb[:, tau, g * 128:(g + 1) * 128], tc3[:, tau, :])],
                         reads=[("F", "dve"), "tcs"], writes=[("ps", sb)])
                srcv = bank(sb)[:, 0:256].rearrange("p (a b e) -> p a b e", a=2, b=NB, e=E)
                extra = ovl_res if (gi == 1) else []
                evac("dve", G5s[gi][:, :, :, tau, :], srcv, reads=[("ps", sb)], writes=[("G", gi)] + extra)

            state = {"pend": None, "s3cur": None}

            def stage23(g, b):
                gi = g % ng
                if b < NB:
                    s2b = (5, 6)[ring("s2bank", 2)]
                    mm_group(bank(s2b)[:, 0:256], [(G3s[gi][:, 0, b, :], cs1), (G3s[gi][:, 1, b, :], cs2)],
                             reads=[("G", gi), "cstb"], writes=[("ps", s2b)])
                    k = ring("Bb", 4)
                    evac("act", Bb[k], bank(s2b)[:, 0:256], reads=[("ps", s2b)], writes=[("Bb", k)])
                if state["pend"] is not None:
                    pb, pk = state["pend"]
                    q4 = pb % 4
                    if q4 == 0:
                        state["s3cur"] = (7, 1)[ring("s3bank", 2)]
                    s3cur = state["s3cur"]
                    mm_group(bank(s3cur)[:, q4 * 128:(q4 + 1) * 128],
                             [(Bb[pk][:, 0:128], mc), (Bb[pk][:, 128:256], ms)],
                             reads=[("Bb", pk), "cstb"], writes=[("ps", s3cur)])
                    if q4 == 3:
                        b4 = pb // 4
                        srcv = bank(s3cur).rearrange("p (b k e) -> p b k e", b=4, k=NT1, e=E)
                        dstv = YT4[:, :, 4 * b4:4 * b4 + 4, :].rearrange("p k b e -> p b k e")
                        evac("dve", dstv, srcv, reads=[("ps", s3cur)], writes=[("YT", "dve")])
                state["pend"] = (b, k) if b < NB else None

            if ng == 1:
                for g in range(4):
                    for tau in range(NT1):
                        stage1(g, tau)
                    for b in range(NB + 1):
                        stage23(g, b)
                    P.op("sp", dma1(yscr[g][:, tok0:tok0 + NT1 * 128], YT[:, 0:NT1 * 128]),
                         reads=[("YT", "dve")], writes=[("yscr", g, tok0)], ndma=1)
            else:
                for tau in range(NT1):
                    stage1(0, tau)
                for g in range(4):
                    for b in range(NB + 1):
                        stage23(g, b)
                        if g + 1 < 4 and b < NT1:
                            stage1(g + 1, b)
                    P.op("sp", dma1(yscr[g][:, tok0:tok0 + NT1 * 128], YT[:, 0:NT1 * 128]),
                         reads=[("YT", "dve")], writes=[("yscr", g, tok0)], ndma=1)

    def phase_b():
        arena.release(base_mark)
        nblk = cfg.get("nblk_b", NBLK)
        win = arena.alloc(8 * 3072, BF16).rearrange("p (k c) -> p k c", c=3072)
        wfo = arena.alloc(4 * 1024, BF16).rearrange("p (g d) -> p g d", d=1024)
        wso = arena.alloc(4 * 1024, BF16).rearrange("p (g d) -> p g d", d=1024)
        wo = arena.alloc(8 * 1024, BF16).rearrange("p (k d) -> p k d", d=1024)
        bmat = arena.alloc(512, F32)
        ones = arena.alloc(128, BF16)
        xt = [arena.alloc(D, F32) for _ in range(4)]
        xs = [arena.alloc(D, BF16) for _ in range(2)]
        hT = arena.alloc(8 * 512, BF16).rearrange("p (k t) -> p k t", t=512)
        uT = arena.alloc(4 * 512, BF16).rearrange("p (c t) -> p c t", t=512)
        taT = arena.alloc(8 * 512, BF16).rearrange("p (c t) -> p c t", t=512)
        tbT = arena.alloc(8 * 512, BF16).rearrange("p (c t) -> p c t", t=512)
        vg = [arena.alloc(512, F32) for _ in range(2)]
        vhat = arena.alloc(4 * 512, BF16).rearrange("p (t c) -> p t c", c=512)
        st6 = [arena.alloc(8, F32) for _ in range(2)]
        mv = [arena.alloc(8, F32) for _ in range(2)]
        Yb = [arena.alloc(4 * 512, BF16).rearrange("p (g t) -> p g t", t=512) for _ in range(2)]
        sT = arena.alloc(4 * 512, BF16).rearrange("p (h t) -> p h t", t=512)
        mT = arena.alloc(8 * 512, BF16).rearrange("p (k t) -> p k t", t=512)
        tmp = [arena.alloc(512, F32) for _ in range(4)]
        tt = [arena.alloc(D, F32) for _ in range(2)]
        xr = [arena.alloc(D, F32) for _ in range(2)]
        gpom = cst[:, C_GPOM:C_GPOM + 1024]
        wstb = cstb[:, B_WST:B_WST + 512].rearrange("p (h q) -> p h q", q=128)

        def ld_win(eng, sem):
            for q in range(3):
                eng.dma_start(out=win[:, :, q * 1024:(q + 1) * 1024],
                              in_=w_in[:, 512 + q * 1024:512 + (q + 1) * 1024].rearrange("(k p) c -> p k c", p=128)
                              ).then_inc(sem, 16)
        P.op("pool", ld_win, writes=["win"], ndma=3)
        P.op("pool", dma1(wfo, w_fo.rearrange("(g p) d -> p g d", p=128)), writes=["wfo"], ndma=1)
        P.op("pool", dma1(wso, w_so.rearrange("(g p) d -> p g d", p=128)), writes=["wso"], ndma=1)
        P.op("pool", dma1(wo, w_o.rearrange("(k p) d -> p k d", p=128)), writes=["wo"], ndma=1)
        P.op("pool", lambda e: e.memset(ones, 1.0), writes=["ones"])
        mm_group(bank(1), [(ones, cstb[:, B_WST:B_WST + 512])], reads=["ones", "cstb"], writes=[("ps", 1)])
        for h in range(4):
            P.op("dve", lambda e, h=h: e.scalar_tensor_tensor(
                out=bmat[:, h * 128:(h + 1) * 128], in0=bank(1)[:, h * 128:(h + 1) * 128],
                scalar=cst[:, C_LNB + h:C_LNB + h + 1], in1=cst[:, C_BS + h * 128:C_BS + (h + 1) * 128],
                op0=ALU.mult, op1=ALU.add), reads=[("ps", 1), "cst"], writes=["bmat"])

        def tiles_for(i):
            t0 = i * BLK
            return [(128, xn[t0 + t * 128:t0 + (t + 1) * 128, :], hT[:, :, t * 128:(t + 1) * 128], "hT")
                    for t in range(4)]

        wring = (1, 2)
        ypairs = ((4, 5), (6, 7))
        U0, V0, GA0, GB0 = 0, 512, 1024, 2048
        st = pre_a("b", tiles_for(0), xt, EPS)
        pre_b(st, xs, (0, 3), gpm)
        def b5(i):
            t0 = i * BLK
            for t in range(4):
                yp = ypairs[ring("ypair", 2)]
                k2 = ring("xr", 2)
                rows = slice(t0 + t * 128, t0 + (t + 1) * 128)
                P.op("sp", dma1(xr[k2], xn[rows, :]), writes=[("xr", k2)], ndma=1)
                for dh in range(2):
                    mm_group(bank(yp[dh]), [(mT[:, kc, t * 128:(t + 1) * 128], wo[:, kc, dh * 512:(dh + 1) * 512])
                                            for kc in range(8)],
                             reads=[("mT", kc) for kc in range(8)] + ["wo"], writes=[("ps", yp[dh])])
                s2 = ring("ss2", 4)
                po = bank(yp[0], 2)
                jk = ring("sqjunk", 2)
                P.op("act", lambda e, po=po, s2=s2, jk=jk: e.activation(out=sqjunks[jk], in_=po, func=AF.Square,
                                                                       accum_out=ssq2[s2][:, 0:1]),
                     reads=[("ps", yp[0]), ("ps", yp[1])], writes=[("sqjunk", jk), ("ssq2", s2)])
                P.op("pool", lambda e, s2=s2: e.tensor_scalar(out=rsd2[s2][:, 0:1], in0=ssq2[s2][:, 0:1],
                                                             scalar1=1.0 / D, scalar2=4.0 * EPS, op0=ALU.mult,
                                                             op1=ALU.add),
                     reads=[("ssq2", s2)], writes=[("rsd2", s2)])
                P.op("pool", lambda e, s2=s2: e.tensor_tensor(out=rsd2[s2][:, 0:1], in0=rsd2[s2][:, 0:1],
                                                             in1=neghalf[:, 0:1], op=ALU.pow),
                     reads=[("rsd2", s2), "neghalf"], writes=[("rsd2", s2)])
                k = ring("tt", 2)
                P.op("dve", lambda e, po=po, s2=s2, k=k: e.scalar_tensor_tensor(
                    out=tt[k], in0=po, scalar=rsd2[s2][:, 0:1], in1=gpom, op0=ALU.mult, op1=ALU.mult),
                    reads=[("ps", yp[0]), ("ps", yp[1]), ("rsd2", s2), "cst"], writes=[("tt", k)])
                P.op(ADD_ENG, lambda e, k=k, k2=k2: e.tensor_tensor(out=tt[k], in0=tt[k], in1=xr[k2], op=ALU.add),
                     reads=[("tt", k), ("xr", k2)], writes=[("tt", k)])
                P.op(ST_ENG, dma1(x1s[rows, :], tt[k]), reads=[("tt", k)], writes=[("x1s", i, t)], ndma=1)


        for i in range(nblk):
            t0 = i * BLK
            yb = Yb[i % 2]
            P.op("sp", dma1(yb, yscr[:, :, t0:t0 + BLK].rearrange("g p t -> p g t")), reads=[("yscr",)],
                 writes=[("Yb", i % 2)], ndma=1)
            for t in range(4):
                wb = wring[ring("wring", 2)]
                mm_group(bank(wb), [(hT[:, kc, t * 128:(t + 1) * 128], win[:, kc, V0:V0 + 512]) for kc in range(8)],
                         reads=["hT", "win"], writes=[("ps", wb)])
                k = ring("vg", 2)
                P.op("act", lambda e, k=k, wb=wb: e.activation(out=vg[k], in_=bank(wb), func=AF.Gelu_apprx_tanh),
                     reads=[("ps", wb)], writes=[("vg", k)])
                P.op("dve", lambda e, k=k: e.bn_stats(out=st6[k][:, 0:6], in_=vg[k]), reads=[("vg", k)],
                     writes=[("st6", k)])
                P.op("dve", lambda e, k=k: e.bn_aggr(out=mv[k][:, 0:2], in_=st6[k][:, 0:6]), reads=[("st6", k)],
                     writes=[("mv", k)])
                P.op("pool", lambda e, k=k: e.tensor_scalar(out=mv[k][:, 2:3], in0=mv[k][:, 1:2], scalar1=1.0,
                                                           scalar2=EPS, op0=ALU.mult, op1=ALU.add),
                     reads=[("mv", k)], writes=[("mvr", k)])
                P.op("pool", lambda e, k=k: e.tensor_tensor(out=mv[k][:, 2:3], in0=mv[k][:, 2:3],
                                                           in1=neghalf[:, 0:1], op=ALU.pow),
                     reads=[("mvr", k), "neghalf"], writes=[("mvr", k)])
                P.op("dve", lambda e, k=k, t=t: e.tensor_scalar(out=vhat[:, t, :], in0=vg[k], scalar1=mv[k][:, 0:1],
                                                               scalar2=mv[k][:, 2:3], op0=ALU.subtract,
                                                               op1=ALU.mult),
                     reads=[("vg", k), ("mv", k), ("mvr", k)], writes=[("vhat", t)])
            if i > 0:
                b5(i - 1)
            def proj_chunk(off, c, dst, dname, func, scale):
                wb = wring[ring("wring", 2)]
                mm_group(bank(wb), [(win[:, kc, off + c * 128:off + (c + 1) * 128], hT[:, kc, :])
                                    for kc in range(8)],
                         reads=["hT", "win"], writes=[("ps", wb)])
                P.op("act", lambda e: e.activation(out=dst[:, c, :], in_=bank(wb), func=func, scale=scale),
                     reads=[("ps", wb)], writes=[(dname, c)])

            def sgu_group(h):
                wb = wring[ring("wring", 2)]
                for t in range(4):
                    mm_group(bank(wb)[:, t * 128:(t + 1) * 128],
                             [(vhat[:, t, h * 128:(h + 1) * 128], wstb[:, h, :])],
                             reads=[("vhat", t), "cstb"], writes=[("ps", wb)])
                k = ring("tmp", 4)
                P.op("dve", lambda e: e.scalar_tensor_tensor(
                    out=tmp[k].rearrange("p (t q) -> p t q", q=128),
                    in0=bank(wb).rearrange("p (t q) -> p t q", q=128),
                    scalar=cst[:, C_LNG + h:C_LNG + h + 1],
                    in1=bmat[:, h * 128:(h + 1) * 128].unsqueeze(1).broadcast_to([128, 4, 128]),
                    op0=ALU.mult, op1=ALU.add),
                    reads=[("ps", wb), "cst", "bmat"], writes=[("tmp", k)])
                P.op("dve", lambda e: e.tensor_tensor(out=sT[:, h, :], in0=tmp[k], in1=uT[:, h, :], op=ALU.mult),
                     reads=[("tmp", k), ("uT", h)], writes=[("sT", h)])

            for c in range(4):
                proj_chunk(U0, c, uT, "uT", AF.Gelu_apprx_tanh, 1.0)
            nxt = None
            for h in range(4):
                sgu_group(h)
                for c in range(4):
                    cc = (h % 2) * 4 + c
                    if h < 2:
                        proj_chunk(GA0, cc, taT, "taT", AF.Tanh, 0.5)
                    else:
                        proj_chunk(GB0, cc, tbT, "tbT", AF.Tanh, 0.5)
            if i + 1 < nblk:
                nxt = pre_a("b", tiles_for(i + 1), xt, EPS)
            for dc in range(8):
                yp = ypairs[ring("ypair", 2)]
                mm_group(bank(yp[0]), [(wfo[:, g, dc * 128:(dc + 1) * 128], yb[:, g, :]) for g in range(4)],
                         reads=["wfo", ("Yb", i % 2)], writes=[("ps", yp[0])])
                mm_group(bank(yp[1]), [(wso[:, h, dc * 128:(dc + 1) * 128], sT[:, h, :]) for h in range(4)],
                         reads=["wso"] + [("sT", h) for h in range(4)], writes=[("ps", yp[1])])
                k1 = ring("tmp", 4)
                P.op("dve", lambda e, k1=k1, dc=dc, yp=yp: e.scalar_tensor_tensor(
                    out=tmp[k1], in0=taT[:, dc, :], scalar=1.0, in1=bank(yp[0]), op0=ALU.add, op1=ALU.mult),
                    reads=[("taT", dc), ("ps", yp[0])], writes=[("tmp", k1)])
                k2 = ring("tmp", 4)
                P.op("dve", lambda e, k2=k2, dc=dc, yp=yp: e.scalar_tensor_tensor(
                    out=tmp[k2], in0=tbT[:, dc, :], scalar=1.0, in1=bank(yp[1]), op0=ALU.add, op1=ALU.mult),
                    reads=[("tbT", dc), ("ps", yp[1])], writes=[("tmp", k2)])
                P.op(ADD_ENG, lambda e, k1=k1, k2=k2, dc=dc: e.tensor_tensor(out=mT[:, dc, :], in0=tmp[k1],
                                                                          in1=tmp[k2], op=ALU.add),
                     reads=[("tmp", k1), ("tmp", k2)], writes=[("mT", dc)])
            if nxt is not None:
                pre_b(nxt, xs, (0, 3), gpm)
        b5(nblk - 1)

    def phase_c(src):
        arena.release(base_mark)
        nblk = cfg.get("nblk", NBLK)
        NH = 2 * NBLK
        wd = arena.alloc(32 * 1024, BF16).rearrange("p (j d) -> p j d", d=1024)
        NW = 3
        wup = [arena.alloc(2048, BF16).rearrange("p (kc h f) -> p kc h f", kc=8, h=2) for _ in range(NW)]
        xt = [arena.alloc(D, F32) for _ in range(4)]
        xs = [arena.alloc(D, BF16) for _ in range(2)]
        h2T = [arena.alloc(8 * 512, BF16).rearrange("p (k t) -> p k t", t=512) for _ in range(2)]
        actT = arena.alloc(32 * 512, BF16).rearrange("p (j t) -> p j t", t=512)
        cgb = [arena.alloc(512, F32) for _ in range(3)]
        cvb = [arena.alloc(512, F32) for _ in range(3)]
        ggb = [arena.alloc(512, BF16) for _ in range(3)]
        tt = [arena.alloc(D, F32) for _ in range(2)]
        xr = [arena.alloc(D, F32) for _ in range(2)]
        haloU = arena.alloc(64 * NH, F32).rearrange("p (c t) -> p c t", t=NH)
        hhT = arena.alloc(8 * NH, BF16).rearrange("p (k t) -> p k t", t=NH)
        gpof = cst[:, C_GPOF:C_GPOF + 1024]

        P.op("pool", dma1(wd, w_down.rearrange("(j p) d -> p j d", p=128)), writes=["wd"], ndma=1)

        def wload(sl, j):
            P.op("sp", dma1(wup[sl], wups[j].rearrange("p (kc h f) -> p kc h f", kc=8, h=2)),
                 reads=[("wups", j)], writes=[("wup", sl)], ndma=1)

        srcb = src.rearrange("(i r) d -> i r d", r=BLK)
        pieces = [(0, 1, src[0:1, :]),
                  (1, NBLK, srcb[0:NBLK - 1, BLK - 1, :]),
                  (NBLK, 2 * NBLK - 1, srcb[1:NBLK, 0, :]),
                  (2 * NBLK - 1, 2 * NBLK, src[NTOK - 1:NTOK, :])]
        st = pre_a("c", [(NH, pieces, hhT[:, :, :], "hhT")], xt, EPS, hm_col=cst[:, C_HM:C_HM + 1])
        pre_b(st, xs, 0, gpf)
        def tiles_for(i):
            t0 = i * BLK
            hb = h2T[i % 2]
            res = ("h2T", i % 2)
            return [(128, src[t0 + t * 128:t0 + (t + 1) * 128, :], hb[:, :, t * 128:(t + 1) * 128], res)
                    for t in range(4)]

        st0 = pre_a("c", tiles_for(0), xt, EPS)
        pre_b(st0, xs, 0, gpf)
        hbanks = [6, 7]
        PER = 12
        NHW = 8
        actflat = actT.rearrange("p j t -> p (j t)")
        hw = [actflat[:, sl * 2048:(sl + 1) * 2048].rearrange("p (kc h f) -> p kc h f", kc=8, h=2) for sl in range(NHW)]

        def hload(sl, j):
            P.op("sp", dma1(hw[sl], wups[j].rearrange("p (kc h f) -> p kc h f", kc=8, h=2)),
                 reads=[("wups", j)], writes=[("hw", sl)], ndma=1)
        for j in range(NHW):
            hload(j % NHW, j)
        for c0 in range(0, 64, PER):
            hbk = hbanks[(c0 // PER) % 2]
            cs_ = list(range(c0, min(c0 + PER, 64)))
            for c in cs_:
                j, half = c // 2, c % 2
                sl = j % NHW
                mm_group(bank(hbk)[:, (c - c0) * NH:(c - c0 + 1) * NH],
                         [(hw[sl][:, kc, half, :], hhT[:, kc, :]) for kc in range(8)],
                         reads=[("hw", sl), "hhT"], writes=[("ps", hbk)])
                if half == 1 and j + NHW < 32:
                    hload((j + NHW) % NHW, j + NHW)
            n = len(cs_)
            for half in range(2):
                idx = [ci for ci in cs_ if ci % 2 == half]
                j0 = idx[0] // 2
                srcv = bank(hbk)[:, 0:n * NH].rearrange("p (c t) -> p c t", t=NH)
                first = idx[0] - c0
                evac("dve" if half == 0 else "act",
                     haloU[:, half * 32 + j0:half * 32 + j0 + len(idx), :],
                     srcv[:, first:n:2, :], reads=[("ps", hbk)], writes=["haloU"])

        cw3 = cst[:, C_CW:C_CW + 192].rearrange("p (c t) -> p c t", t=3)
        for side in range(2):
            hv = haloU[:, :, side * NBLK:(side + 1) * NBLK]
            wsd = cw3[:, :, 2 * side:2 * side + 1].broadcast_to([128, 64, NBLK])
            cbb = cst[:, C_CB:C_CB + 64].unsqueeze(2).broadcast_to([128, 64, NBLK])
            P.op("dve", lambda e, hv=hv, wsd=wsd: e.tensor_tensor(out=hv, in0=hv, in1=wsd, op=ALU.mult),
                 reads=["haloU", "cst"], writes=["haloU"])
            P.op("dve", lambda e, hv=hv, cbb=cbb: e.tensor_tensor(out=hv, in0=hv, in1=cbb, op=ALU.add),
                 reads=["haloU", "cst"], writes=["haloU"])

        chunks = [(i, j) for i in range(nblk) for j in range(32)]
        loaded = [0]
        wslot = {}

        def ensure_loaded(upto):
            while loaded[0] <= min(upto, len(chunks) - 1):
                n = loaded[0]
                i, j = chunks[n]
                wslot[n] = n % NW
                wload(n % NW, j)
                loaded[0] += 1

        uppairs = [(1, 2), (3, 4), (5, 6)]
        pdslots = [5, 1, 3]

        pending_fin = [None]
        deferred_epi = [None]
        for i in range(nblk):
            hb = h2T[i % 2]
            hres = ("h2T", i % 2)
            nxt = None
            for j in range(32):
                n = i * 32 + j
                ensure_loaded(n + NW - 1)
                sl = wslot[n]
                w = wup[sl]
                pr = uppairs[ring("uppair", 3)]
                for half in range(2):
                    b = pr[half]
                    mm_group(bank(b), [(w[:, kc, half, :], hb[:, kc, :]) for kc in range(8)],
                             reads=[("wup", sl), hres], writes=[("ps", b)])
                k = ring("cg", 3)
                cg, cv, gg = cgb[k], cvb[k], ggb[k]
                trip = ((0, cg, "cg"), (1, cv, "cv"))
                for half, cbuf, cn in trip:
                    b = pr[half]
                    c = half * 32 + j
                    w1 = cst[:, C_CW + 3 * c + 1:C_CW + 3 * c + 2]
                    bb = cst[:, C_CB + c:C_CB + c + 1]
                    P.op("act", lambda e, cbuf=cbuf, b=b, w1=w1, bb=bb: e.activation(
                        out=cbuf[:, 1:511], in_=bank(b)[:, 1:511], func=AF.Identity, bias=bb, scale=w1),
                        reads=[("ps", b), "cst"], writes=[(cn, k, "m")])
                    for side in range(2):
                        col = 511 * side
                        hv = haloU[:, c, side * NBLK + i:side * NBLK + i + 1]
                        P.op("act", lambda e, cbuf=cbuf, b=b, w1=w1, hv=hv, col=col: e.activation(
                            out=cbuf[:, col:col + 1], in_=bank(b)[:, col:col + 1], func=AF.Identity, bias=hv,
                            scale=w1),
                            reads=[("ps", b), "cst", "haloU"], writes=[(cn, k, side)])
                for side in range(2):
                    for half, cbuf, cn in trip:
                        b = pr[half]
                        c = half * 32 + j
                        wt = cst[:, C_CW + 3 * c + 2 * side:C_CW + 3 * c + 2 * side + 1]
                        pb = bank(b)
                        o_ = cbuf[:, 1:512] if side == 0 else cbuf[:, 0:511]
                        i_ = pb[:, 0:511] if side == 0 else pb[:, 1:512]
                        rr = [(cn, k, "m"), (cn, k, 1 - side)]
                        P.op("dve", lambda e, o_=o_, i_=i_, wt=wt: e.scalar_tensor_tensor(
                            out=o_, in0=i_, scalar=wt, in1=o_, op0=ALU.mult, op1=ALU.add),
                            reads=[("ps", b), "cst"] + rr, writes=rr)
                def finish(k=k, cg=cg, cv=cv, gg=gg, j=j):
                    P.op("act", lambda e: e.activation(out=gg, in_=cg, func=AF.Gelu_apprx_tanh),
                         reads=[("cg", k, "m"), ("cg", k, 0), ("cg", k, 1)], writes=[("gg", k)])
                    P.op(MUL_ENG, lambda e: e.tensor_tensor(out=actT[:, j, :], in0=gg, in1=cv, op=ALU.mult),
                         reads=[("gg", k), ("cv", k, "m"), ("cv", k, 0), ("cv", k, 1)], writes=[("act", j)])
                if pending_fin[0] is not None:
                    pending_fin[0]()
                pending_fin[0] = finish
                if j == 0 and deferred_epi[0] is not None:
                    deferred_epi[0]()
                    deferred_epi[0] = None
            pending_fin[0]()
            pending_fin[0] = None
            t0 = i * BLK
            if i + 1 < nblk:
                nxt = pre_a("c", tiles_for(i + 1), xt, EPS)
            r = ring_ctr.get("uppair", 0) % 3
            tbanks = [uppairs[r][0], uppairs[(r + 1) % 3][0], uppairs[(r + 2) % 3][0], uppairs[(r + 2) % 3][0]]

            def dpart(t, j0, j1):
                b0 = tbanks[t]
                for dh in range(2):
                    P.op("pe", lambda e, t=t, dh=dh, b0=b0, j0=j0, j1=j1: [e.matmul(
                        bank(b0 + dh), lhsT=actT[:, j, t * 128:(t + 1) * 128],
                        rhs=wd[:, j, dh * 512:(dh + 1) * 512], start=(j == 0), stop=(j == 31))
                        for j in range(j0, j1)][-1],
                        reads=[("act", j) for j in range(j0, j1)] + ["wd"], writes=[("ps", b0 + dh)])

            def depi(t):
                b0 = tbanks[t]
                k2 = ring("xr", 2)
                rows = slice(t0 + t * 128, t0 + (t + 1) * 128)
                P.op("sp", dma1(xr[k2], src[rows, :]), writes=[("xr", k2)], ndma=1)
                s2 = ring("ss2", 4)
                pd = bank(b0, 2)
                jk = ring("sqjunk", 2)
                P.op("act", lambda e: e.activation(out=sqjunks[jk], in_=pd, func=AF.Square,
                                                   accum_out=ssq2[s2][:, 0:1]),
                     reads=[("ps", b0), ("ps", b0 + 1)], writes=[("sqjunk", jk), ("ssq2", s2)])
                P.op("pool", lambda e: e.tensor_scalar(out=rsd2[s2][:, 0:1], in0=ssq2[s2][:, 0:1],
                                                       scalar1=1.0 / D, scalar2=EPS, op0=ALU.mult, op1=ALU.add),
                     reads=[("ssq2", s2)], writes=[("rsd2", s2)])
                P.op("pool", lambda e: e.tensor_tensor(out=rsd2[s2][:, 0:1], in0=rsd2[s2][:, 0:1],
                                                       in1=neghalf[:, 0:1], op=ALU.pow),
                     reads=[("rsd2", s2), "neghalf"], writes=[("rsd2", s2)])
                def tail():
                    k = ring("tt", 2)
                    P.op("dve", lambda e: e.scalar_tensor_tensor(
                        out=tt[k], in0=pd, scalar=rsd2[s2][:, 0:1], in1=gpof, op0=ALU.mult, op1=ALU.mult),
                        reads=[("ps", b0), ("ps", b0 + 1), ("rsd2", s2), "cst"], writes=[("tt", k)])
                    P.op(ADD_ENG, lambda e: e.tensor_tensor(out=tt[k], in0=tt[k], in1=xr[k2], op=ALU.add),
                         reads=[("tt", k), ("xr", k2)], writes=[("tt", k)])
                    P.op(ST_ENG, dma1(y[rows, :], tt[k]), reads=[("tt", k)], writes=[("y", i, t)], ndma=1)
                if t == 3:
                    deferred_epi[0] = tail
                else:
                    tail()

            JA = 22
            for t in range(3):
                dpart(t, 0, JA)
                if t == 1 and nxt is not None:
                    pre_b(nxt, xs, 0, gpf)
            cuts = [JA, 26, 28, 30, 31, 32]
            for t in (2, 0, 1):
                for ci in range(len(cuts) - 1):
                    dpart(t, cuts[ci], cuts[ci + 1])
                depi(t)
            dpart(3, 0, 32)
            depi(3)
        if deferred_epi[0] is not None:
            deferred_epi[0]()
            deferred_epi[0] = None

    if "A" in phases:
        phase_a()
        P.barrier()
    if "B" in phases:
        phase_b()
        P.barrier()
    if "C" in phases:
        if "A" not in phases:
            cast_wup()
        phase_c(xn if cfg.get("c_src") == "xn" else x1s)
    P.emit()
    return nc


def _core_kind(c):
    return "sample" if c < 2 else "prompt"


def _core_tokens(c, x_prompt, x_sample):
    if c < 2:
        return np.concatenate([x_sample[c], x_prompt[c]], axis=0)
    b0 = 2 + 5 * (c - 2)
    return x_prompt[b0:b0 + 5].reshape(NTOK, D)


_TABLES = {}


def _tables(kind):
    if kind not in _TABLES:
        _TABLES[kind] = _dft_tables(kind)
    return _TABLES[kind]


def make_in_maps(inputs, cores):
    f = lambda a: np.ascontiguousarray(np.asarray(a, dtype=np.float32))
    x_prompt = f(inputs["x_prompt"])
    x_sample = f(inputs["x_sample"])
    w_in = f(inputs["w_in"][0])
    shared = {
        "w_in": w_in,
        "w_fo": f(inputs["w_fourier_out"][0]),
        "w_so": f(inputs["w_sgu_out"][0]),
        "w_o": f(inputs["w_o"][0]),
        "w_up": f(inputs["w_up"][0]),
        "w_down": f(inputs["w_down"][0]),
    }
    cst0 = np.zeros((128, NCST), dtype=np.float32)
    col = lambda v: f(v).reshape(8, 128).T
    cst0[:, C_GPM:C_GPM + 8] = col(inputs["norm_pre_mix"][0])
    cst0[:, C_GPF:C_GPF + 8] = col(inputs["norm_pre_ffn"][0])
    cst0[:, C_GPOM:C_GPOM + 1024] = f(inputs["norm_post_mix"][0])[None, :]
    cst0[:, C_GPOF:C_GPOF + 1024] = f(inputs["norm_post_ffn"][0])[None, :]
    cw = f(inputs["conv_w"][0])
    cst0[:, C_CW:C_CW + 192] = cw.reshape(3, 64, 128).transpose(2, 1, 0).reshape(128, 192)
    cst0[:, C_CB:C_CB + 64] = f(inputs["conv_b"][0]).reshape(64, 128).T
    cst0[:, C_LNG:C_LNG + 4] = f(inputs["sgu_ln_g"][0]).reshape(4, 128).T
    cst0[:, C_LNB:C_LNB + 4] = f(inputs["sgu_ln_b"][0]).reshape(4, 128).T
    cst0[:, C_BS:C_BS + 512] = f(inputs["sgu_b_s"][0]).reshape(1, 512)
    wst = f(inputs["sgu_w_s"][0]).transpose(2, 0, 1).reshape(128, 512)
    cst0[:, C_WST:C_WST + 512] = wst
    maps = []
    for c in cores:
        kind = _core_kind(c)
        t8, t2, cs, m8, m2 = _tables(kind)
        xn = np.ascontiguousarray(_core_tokens(c, x_prompt, x_sample))
        xfo = np.ascontiguousarray(xn[_fourier_row_order(kind)])
        cst = cst0.copy()
        cst[:, C_HM:C_HM + 1] = _halo_mask(kind)
        cstb = np.zeros((128, NCSTB), dtype=np.float32)
        cstb[:, B_ID:B_ID + 128] = np.eye(128, dtype=np.float32)
        cstb[:, B_CS:B_CS + 512] = cs
        cstb[:, B_M8:B_M8 + 256] = m8
        cstb[:, B_M2:B_M2 + 256] = m2
        cstb[:, B_WST:B_WST + 512] = wst
        m = dict(shared)
        m.update({"xn": xn, "xf": xfo, "cst": cst, "cstb": cstb, "t8": t8, "t2": t2})
        maps.append(m)
    return maps


def assemble_outputs(ys):
    y_prompt = np.empty((32, 2048, D), dtype=np.float32)
    y_sample = np.empty((2, 8192, D), dtype=np.float32)
    for c in range(NCORES):
        yc = ys[c]
        if c < 2:
            y_sample[c] = yc[:S8]
            y_prompt[c] = yc[S8:]
        else:
            b0 = 2 + 5 * (c - 2)
            y_prompt[b0:b0 + 5] = yc.reshape(5, 2048, D)
    return y_prompt, y_sample


def kernel(**inputs):
    nc = build_program()
    maps = make_in_maps(inputs, list(range(NCORES)))
    res = run_bass_kernel_spmd(nc, maps, core_ids=list(range(NCORES)))
    ys = [np.asarray(r["y"], dtype=np.float32) for r in res.results]
    return assemble_outputs(ys)
```
